# Optimizing a Trainium2 kernel written in Bass

```python
import jax
import jax.numpy as jnp
from jax import lax
import numpy as np


D_MODEL = 2048
BATCH = 1
SEQ = 8192
DEPTH = 4

GRID_W = 64
CTX_LEN = 256
N_EVEN = (DEPTH + 1) // 2
N_ODD = DEPTH // 2
EPS = 1e-6
ROPE_THETA = 10000.0
ROPE_DIM = 64
QBLOCK = 128

POOL_WINDOWS = (2, 4, 8, 16)
POOL_GROUPS = 4
POOL_W = D_MODEL // 4
POOL_GC = POOL_W // POOL_GROUPS
MLA_HEADS = 12
MLA_NOPE = 128
MLA_VDIM = 128
MLA_QK = MLA_NOPE + ROPE_DIM
MLA_W = MLA_HEADS * MLA_VDIM
Q_RANK = D_MODEL // 4
KV_RANK = D_MODEL // 4
M_HEADS = 4
M_DQK = 128
M_DV = 256
M_QK_W = M_HEADS * M_DQK
M_V_W = M_HEADS * M_DV
M_CHUNK = 64
A_HEADS = 16
A_KV_HEADS = 2
A_HD = 64
A_GROUP = A_HEADS // A_KV_HEADS
A_W = A_HEADS * A_HD
A_KV_W = A_KV_HEADS * A_HD
WINDOW = 128

EV_SIZES = (POOL_W, POOL_W, Q_RANK, KV_RANK, ROPE_DIM, MLA_W)
EV_IN = POOL_W + POOL_W + Q_RANK + KV_RANK + ROPE_DIM + MLA_W
OD_SIZES = (M_QK_W, M_QK_W, M_V_W, M_V_W, 4 * M_HEADS, M_V_W, A_W, A_KV_W, A_KV_W, A_W)
OD_IN = 2 * M_QK_W + 3 * M_V_W + 4 * M_HEADS + 2 * A_W + 2 * A_KV_W
EV_OUT = POOL_W + MLA_W
OD_OUT = M_V_W + A_W

kernel_name = 'hybrid_pool_mla_mlstm_swa_prefix_dit'

F32 = jnp.float32


def rmsnorm(x, w):
    x32 = x.astype(F32)
    y = x32 * lax.rsqrt(jnp.mean(x32 * x32, axis=-1, keepdims=True) + EPS)
    return y.astype(x.dtype) * w


def split_cols(u, sizes):
    idx = np.cumsum(np.array(sizes))[:-1].tolist()
    return jnp.split(u, idx, axis=-1)


def axial_angles(seq_len):
    rows = seq_len // GRID_W
    row, col = jnp.meshgrid(jnp.arange(rows), jnp.arange(GRID_W), indexing='ij')
    row = row.reshape(-1).astype(F32)
    col = col.reshape(-1).astype(F32)
    half = ROPE_DIM // 2
    inv = ROPE_THETA ** (-jnp.arange(0, half, 2, dtype=F32) / half)
    return row[:, None, None] * inv, col[:, None, None] * inv


def rope_axis(x, ang):
    x1, x2 = jnp.split(x, 2, axis=-1)
    cos = jnp.cos(ang).astype(x.dtype)
    sin = jnp.sin(ang).astype(x.dtype)
    return jnp.concatenate([x1 * cos - x2 * sin, x1 * sin + x2 * cos], axis=-1)


def rope_2d(x, ang_r, ang_c):
    xr, xc = jnp.split(x, 2, axis=-1)
    return jnp.concatenate([rope_axis(xr, ang_r), rope_axis(xc, ang_c)], axis=-1)


def pool_branch(u, w_pool, pool_scale):
    B, L, _ = u.shape
    u32 = u.astype(F32)
    csum = jnp.pad(jnp.cumsum(u32, axis=1), ((0, 0), (1, 0), (0, 0)))
    t = jnp.arange(L)
    groups = []
    for g, w in enumerate(POOL_WINDOWS):
        lo = jnp.clip(t - w // 2, 0, L - 1)
        hi = jnp.clip(t + w // 2 - 1, 0, L - 1)
        cs = csum[:, :, g * POOL_GC:(g + 1) * POOL_GC]
        mean = (cs[:, hi + 1] - cs[:, lo]) / (hi - lo + 1).astype(F32)[None, :, None]
        groups.append(mean - u32[:, :, g * POOL_GC:(g + 1) * POOL_GC])
    z = jnp.stack(groups, axis=2).astype(u.dtype)
    z = jnp.einsum('blgc,gcd->blgd', z, w_pool).reshape(B, L, POOL_W)
    return z * pool_scale


def mla_q(c_q, q_norm, w_uq, ang):
    B, L, _ = c_q.shape
    q = (rmsnorm(c_q, q_norm) @ w_uq).reshape(B, L, MLA_HEADS, MLA_QK)
    q_nope, q_rope = q[..., :MLA_NOPE], q[..., MLA_NOPE:]
    if ang is not None:
        q_rope = rope_2d(q_rope, *ang)
    return jnp.concatenate([q_nope, q_rope], axis=-1)


def mla_kv(c_kv, k_r, kv_norm, w_ukv, ang):
    B, L, _ = c_kv.shape
    kv = (rmsnorm(c_kv, kv_norm) @ w_ukv).reshape(B, L, MLA_HEADS, MLA_NOPE + MLA_VDIM)
    k_nope, v = kv[..., :MLA_NOPE], kv[..., MLA_NOPE:]
    k_rope = k_r[:, :, None, :]
    if ang is not None:
        k_rope = rope_2d(k_rope, *ang)
    k = jnp.concatenate([k_nope, jnp.broadcast_to(k_rope, (B, L, MLA_HEADS, ROPE_DIM))], axis=-1)
    return k, v


def block_dense_attention(q, k, v):
    B, S, H, d = q.shape
    scale = d ** -0.5
    qb = q.reshape(B, S // QBLOCK, QBLOCK, H, d).transpose(1, 0, 2, 3, 4)

    def one_block(qi):
        s = jnp.einsum('bqhd,bkhd->bhqk', qi, k).astype(F32) * scale
        p = jax.nn.softmax(s, axis=-1).astype(v.dtype)
        return jnp.einsum('bhqk,bkhd->bqhd', p, v)

    o = lax.map(one_block, qb)
    return o.transpose(1, 0, 2, 3, 4).reshape(B, S, H, v.shape[-1])


def even_mix(h, hc, w_in, q_norm, kv_norm, w_uq, w_ukv, w_pool, pool_scale, w_out, ang, need_ctx):
    B, L, _ = h.shape
    Lc = hc.shape[1]
    p_in, p_gate, c_q, c_kv, k_r, a_gate = split_cols(h @ w_in, EV_SIZES)
    pc_in, pc_gate, cc_q, cc_kv, kc_r, ac_gate = split_cols(hc @ w_in, EV_SIZES)
    q = mla_q(c_q, q_norm, w_uq, ang)
    k, v = mla_kv(c_kv, k_r, kv_norm, w_ukv, ang)
    kc, vc = mla_kv(cc_kv, kc_r, kv_norm, w_ukv, None)
    o = block_dense_attention(q, jnp.concatenate([k, kc], axis=1), jnp.concatenate([v, vc], axis=1))
    y = jnp.concatenate([pool_branch(p_in, w_pool, pool_scale) * jax.nn.silu(p_gate),
                         o.reshape(B, L, MLA_W) * jax.nn.silu(a_gate)], axis=-1) @ w_out
    if not need_ctx:
        return y, None
    qc = mla_q(cc_q, q_norm, w_uq, None)
    oc = block_dense_attention(qc, kc, vc)
    yc = jnp.concatenate([pool_branch(pc_in, w_pool, pool_scale) * jax.nn.silu(pc_gate),
                          oc.reshape(B, Lc, MLA_W) * jax.nn.silu(ac_gate)], axis=-1) @ w_out
    return y, yc


def mlstm_init_state(B):
    return (jnp.zeros((B, M_HEADS, M_DV, M_DQK), F32),
            jnp.zeros((B, M_HEADS, M_DQK), F32),
            jnp.zeros((B, M_HEADS), F32))


def mlstm_scan(q, k, v, ig, lf, state, return_h):
    B, H, L, _ = q.shape
    dv = v.shape[-1]
    nc = L // M_CHUNK

    def chunks(a):
        return jnp.moveaxis(a.reshape((B, H, nc, M_CHUNK) + a.shape[3:]), 2, 0)

    tril = jnp.tril(jnp.ones((M_CHUNK, M_CHUNK), dtype=bool))

    def step(carry, inp):
        C, n, m = carry
        qc, kc, vc, ic, fc = inp
        b = jnp.cumsum(fc, axis=-1)
        b_last = b[..., -1]
        a = b_last[..., None] - b + ic
        m_new = jnp.maximum(b_last + m, a.max(axis=-1))
        decay = jnp.exp(b_last + m - m_new)
        w = jnp.exp(a - m_new[..., None])
        C_new = decay[..., None, None] * C + jnp.einsum('bhs,bhsv,bhsk->bhvk', w, vc, kc)
        n_new = decay[..., None] * n + jnp.einsum('bhs,bhsk->bhk', w, kc)
        if not return_h:
            return (C_new, n_new, m_new), None
        dlog = jnp.where(tril, b[..., :, None] - b[..., None, :] + ic[..., None, :], -jnp.inf)
        inter = b + m[..., None]
        m_t = jnp.maximum(inter, dlog.max(axis=-1))
        iw = jnp.exp(inter - m_t)
        s = jnp.einsum('bhtk,bhsk->bhts', qc, kc) * jnp.exp(dlog - m_t[..., None])
        num = iw[..., None] * jnp.einsum('bhvk,bhtk->bhtv', C, qc) + jnp.einsum('bhts,bhsv->bhtv', s, vc)
        den = iw * jnp.einsum('bhk,bhtk->bht', n, qc) + s.sum(axis=-1)
        hout = num / jnp.maximum(jnp.abs(den), jnp.exp(-m_t))[..., None]
        return (C_new, n_new, m_new), hout

    final, hs = lax.scan(step, state, (chunks(q), chunks(k), chunks(v), chunks(ig), chunks(lf)))
    if not return_h:
        return None, final
    return jnp.moveaxis(hs, 0, 2).reshape(B, H, L, dv), final


def mlstm_prep(mq, mk, mv, mg, gate_b):
    B, L, _ = mq.shape
    q = mq.reshape(B, L, M_HEADS, M_DQK).transpose(0, 2, 1, 3).astype(F32) * (M_DQK ** -0.5)
    k = mk.reshape(B, L, M_HEADS, M_DQK).transpose(0, 2, 1, 3).astype(F32)
    v = mv.reshape(B, L, M_HEADS, M_DV).transpose(0, 2, 1, 3).astype(F32)
    g = (mg.astype(F32) + gate_b.astype(F32)).reshape(B, L, 4, M_HEADS).transpose(2, 0, 3, 1)
    return q, k, v, g[0], jax.nn.log_sigmoid(g[1]), g[2], jax.nn.log_sigmoid(g[3])


def bidir_mlstm(lat, ctxp, need_ctx):
    q, k, v, i_f, l_f, i_b, l_b = lat
    cq, ck, cv, ci_f, cl_f, ci_b, cl_b = ctxp
    flip = lambda a: jnp.flip(a, axis=2)
    zero = mlstm_init_state(q.shape[0])
    hcf, s_f = mlstm_scan(cq, ck, cv, ci_f, cl_f, zero, need_ctx)
    hcb, s_b = mlstm_scan(flip(cq), flip(ck), flip(cv), flip(ci_b), flip(cl_b), zero, need_ctx)
    hf, _ = mlstm_scan(q, k, v, i_f, l_f, s_f, True)
    hb, _ = mlstm_scan(flip(q), flip(k), flip(v), flip(i_b), flip(l_b), s_b, True)
    h = hf + flip(hb)
    hc = hcf + flip(hcb) if need_ctx else None
    return h, hc


def mlstm_out(hsum, o, z, head_norm):
    B, H, L, dv = hsum.shape
    hn = rmsnorm(hsum.transpose(0, 2, 1, 3), head_norm.reshape(M_HEADS, M_DV)).reshape(B, L, M_V_W)
    return hn * jax.nn.sigmoid(o) * jax.nn.silu(z)


def window_attention(q, k, v, kc, vc, sink):
    B, L = q.shape[:2]
    Lc = kc.shape[1]
    nb = L // WINDOW
    scale = A_HD ** -0.5
    qb = q.reshape(B, nb, WINDOW, A_KV_HEADS, A_GROUP, A_HD)

    def band(a):
        ap = jnp.pad(a, ((0, 0), (WINDOW, WINDOW), (0, 0), (0, 0))).reshape(B, nb + 2, WINDOW, A_KV_HEADS, A_HD)
        return jnp.concatenate([ap[:, :-2], ap[:, 1:-1], ap[:, 2:]], axis=2)

    kb, vb = band(k), band(v)
    qi = jnp.arange(WINDOW)[:, None]
    kj = jnp.arange(3 * WINDOW)[None, :]
    kpos = jnp.arange(nb)[:, None, None] * WINDOW + kj[None] - WINDOW
    mask = (jnp.abs(kj - WINDOW - qi) <= WINDOW)[None] & (kpos >= 0) & (kpos < L)
    s_loc = jnp.einsum('bnqhgd,bnkhd->bnhgqk', qb, kb).astype(F32) * scale
    s_loc = jnp.where(mask[None, :, None, None], s_loc, -jnp.inf)
    s_ctx = jnp.einsum('bnqhgd,bchd->bnhgqc', qb, kc).astype(F32) * scale
    s_sink = jnp.broadcast_to(sink.astype(F32)[None, None, :, :, None, None], s_loc.shape[:-1] + (1,))
    p = jax.nn.softmax(jnp.concatenate([s_loc, s_ctx, s_sink], axis=-1), axis=-1).astype(v.dtype)
    p_loc = p[..., :3 * WINDOW]
    p_ctx = p[..., 3 * WINDOW:3 * WINDOW + Lc]
    o = jnp.einsum('bnhgqk,bnkhd->bnqhgd', p_loc, vb) + jnp.einsum('bnhgqc,bchd->bnqhgd', p_ctx, vc)
    return o.reshape(B, L, A_W)


def ctx_gqa(q, k, v, sink):
    B, Lc = q.shape[:2]
    s = jnp.einsum('bqhgd,bkhd->bhgqk', q, k).astype(F32) * (A_HD ** -0.5)
    s_sink = jnp.broadcast_to(sink.astype(F32)[None, :, :, None, None], s.shape[:-1] + (1,))
    p = jax.nn.softmax(jnp.concatenate([s, s_sink], axis=-1), axis=-1)[..., :-1].astype(v.dtype)
    return jnp.einsum('bhgqk,bkhd->bqhgd', p, v).reshape(B, Lc, A_W)


def odd_mix(h, hc, w_in, gate_b, head_norm, sink, w_out, ang, need_ctx):
    B, L, _ = h.shape
    Lc = hc.shape[1]
    mq, mk, mv, mo, mg, mz, aq, ak, av, az = split_cols(h @ w_in, OD_SIZES)
    cmq, cmk, cmv, cmo, cmg, cmz, caq, cak, cav, caz = split_cols(hc @ w_in, OD_SIZES)
    hm, hmc = bidir_mlstm(mlstm_prep(mq, mk, mv, mg, gate_b), mlstm_prep(cmq, cmk, cmv, cmg, gate_b), need_ctx)
    ym = mlstm_out(hm, mo, mz, head_norm)
    sink_g = sink.reshape(A_KV_HEADS, A_GROUP)
    q = rope_2d(aq.reshape(B, L, A_HEADS, A_HD), *ang).reshape(B, L, A_KV_HEADS, A_GROUP, A_HD)
    k = rope_2d(ak.reshape(B, L, A_KV_HEADS, A_HD), *ang)
    v = av.reshape(B, L, A_KV_HEADS, A_HD)
    kc = cak.reshape(B, Lc, A_KV_HEADS, A_HD)
    vc = cav.reshape(B, Lc, A_KV_HEADS, A_HD)
    ya = window_attention(q, k, v, kc, vc, sink_g) * jax.nn.silu(az)
    y = jnp.concatenate([ym, ya], axis=-1) @ w_out
    if not need_ctx:
        return y, None
    ymc = mlstm_out(hmc, cmo, cmz, head_norm)
    qc = caq.reshape(B, Lc, A_KV_HEADS, A_GROUP, A_HD)
    yac = ctx_gqa(qc, kc, vc, sink_g) * jax.nn.silu(caz)
    yc = jnp.concatenate([ymc, yac], axis=-1) @ w_out
    return y, yc


def setup_inputs(seed: int = 0) -> dict:
    key = jax.random.key(seed)
    ks = jax.random.split(key, 24)
    nrm = lambda k, shape, s: jax.random.normal(k, shape, F32) * s
    gain = lambda k, shape: 1.0 + 0.02 * jax.random.normal(k, shape, F32)
    f_bias = jnp.linspace(3.0, 6.0, M_HEADS, dtype=F32)
    gate_off = jnp.concatenate([jnp.zeros((M_HEADS,), F32), f_bias, jnp.zeros((M_HEADS,), F32), f_bias])
    return {
        'x': nrm(ks[0], (BATCH, SEQ, D_MODEL), 1.0),
        'c': nrm(ks[1], (BATCH, D_MODEL), 1.0),
        'ctx': nrm(ks[2], (BATCH, CTX_LEN, D_MODEL), 1.0),
        'c_ctx': nrm(ks[3], (D_MODEL,), 1.0),
        'ada_w': nrm(ks[4], (DEPTH, D_MODEL, 3 * D_MODEL), 0.5 * D_MODEL ** -0.5),
        'ada_b': nrm(ks[5], (DEPTH, 3 * D_MODEL), 0.02),
        'norm_w': gain(ks[6], (DEPTH, D_MODEL)),
        'ev_w_in': nrm(ks[7], (N_EVEN, D_MODEL, EV_IN), D_MODEL ** -0.5),
        'ev_q_norm': gain(ks[8], (N_EVEN, Q_RANK)),
        'ev_kv_norm': gain(ks[9], (N_EVEN, KV_RANK)),
        'ev_w_uq': nrm(ks[10], (N_EVEN, Q_RANK, MLA_HEADS * MLA_QK), Q_RANK ** -0.5),
        'ev_w_ukv': nrm(ks[11], (N_EVEN, KV_RANK, MLA_HEADS * (MLA_NOPE + MLA_VDIM)), KV_RANK ** -0.5),
        'ev_w_pool': nrm(ks[12], (N_EVEN, POOL_GROUPS, POOL_GC, POOL_GC), POOL_GC ** -0.5),
        'ev_pool_scale': gain(ks[13], (N_EVEN, POOL_W)),
        'ev_w_out': nrm(ks[14], (N_EVEN, EV_OUT, D_MODEL), EV_OUT ** -0.5),
        'od_w_in': nrm(ks[15], (N_ODD, D_MODEL, OD_IN), D_MODEL ** -0.5),
        'od_gate_b': gate_off + nrm(ks[16], (N_ODD, 4 * M_HEADS), 0.1),
        'od_head_norm': gain(ks[17], (N_ODD, M_V_W)),
        'od_sink': nrm(ks[18], (N_ODD, A_HEADS), 0.5),
        'od_w_out': nrm(ks[19], (N_ODD, OD_OUT, D_MODEL), OD_OUT ** -0.5),
        'final_norm': gain(ks[20], (D_MODEL,)),
    }


def reference(x, c, ctx, c_ctx, ada_w, ada_b, norm_w, ev_w_in, ev_q_norm, ev_kv_norm, ev_w_uq, ev_w_ukv,
              ev_w_pool, ev_pool_scale, ev_w_out, od_w_in, od_gate_b, od_head_norm, od_sink, od_w_out, final_norm):
    L = x.shape[1]
    ang = axial_angles(L)
    xc = ctx
    sc = jax.nn.silu(c)
    scc = jax.nn.silu(c_ctx)
    for i in range(DEPTH):
        need_ctx = i < DEPTH - 1
        shift, scale, gate = jnp.split((sc @ ada_w[i] + ada_b[i])[:, None, :], 3, axis=-1)
        shift_c, scale_c, gate_c = jnp.split(scc @ ada_w[i] + ada_b[i], 3, axis=-1)
        h = rmsnorm(x, norm_w[i]) * (1.0 + scale) + shift
        hc = rmsnorm(xc, norm_w[i]) * (1.0 + scale_c) + shift_c
        j = i // 2
        if i % 2 == 0:
            y, yc = even_mix(h, hc, ev_w_in[j], ev_q_norm[j], ev_kv_norm[j], ev_w_uq[j], ev_w_ukv[j],
                             ev_w_pool[j], ev_pool_scale[j], ev_w_out[j], ang, need_ctx)
        else:
            y, yc = odd_mix(h, hc, od_w_in[j], od_gate_b[j], od_head_norm[j], od_sink[j], od_w_out[j],
                            ang, need_ctx)
        x = x + gate * y
        if need_ctx:
            xc = xc + gate_c * yc
    return rmsnorm(x, final_norm)
```

```python
import contextlib
import numpy as np
import concourse.bass as bass
import concourse.mybir as mybir
from concourse.bass_utils import run_bass_kernel_spmd

F32 = mybir.dt.float32
BF16 = mybir.dt.bfloat16
ALU = mybir.AluOpType
AF = mybir.ActivationFunctionType
AX = mybir.AxisListType

ENGS = ("pe", "act", "dve", "pool", "sp")
N_DMA_SEMS = 24


class Tile:
    def __init__(self, handle, name, space):
        self.h = handle
        self.name = name
        self.space = space
        self.last_w = None
        self.readers = []

    def __getitem__(self, idx):
        return View(self, self.h[idx])

    def v(self):
        return View(self, self.h if self.space == "dram" else self.h[:])


class View:
    def __init__(self, tile, ap):
        self.tile = tile
        self.ap = ap

    def __getitem__(self, idx):
        return View(self.tile, self.ap[idx])

    def rearrange(self, *a, **k):
        return View(self.tile, self.ap.rearrange(*a, **k))

    def bitcast(self, dt):
        return View(self.tile, self.ap.bitcast(dt))

    def partition_broadcast(self, n):
        return View(self.tile, self.ap.partition_broadcast(n))


class Prog:
    def __init__(self, name="k"):
        self.nc = bass.Bass("TRN2", target_bir_lowering=False, name=name)
        self.stack = contextlib.ExitStack()
        self.ops = {e: [] for e in ENGS}
        self.cnt = {e: 0 for e in ENGS}
        self.esem = {}
        self.waited = {e: {} for e in ENGS}
        self.dma_sems = []
        self.dma_rr = 0
        self.sems = {}
        self.out_tokens = []
        self.n_tiles = 0
        self.pending = {e: [] for e in ENGS}
        self.phase_stack = None
        self.phase_id = 0

    def _sem(self, name):
        s = self.stack.enter_context(self.nc.semaphore(name))
        self.sems[name] = s
        return s

    def setup(self):
        for e in ("pe", "act", "dve", "pool"):
            self.esem[e] = self._sem("e_" + e)
        for i in range(N_DMA_SEMS):
            self.dma_sems.append([self._sem(f"d{i}"), 0, None])

    def begin_phase(self):
        self.phase_stack = contextlib.ExitStack()
        self.phase_id += 1

    def end_phase(self):
        self.barrier()
        self.phase_stack.close()
        self.phase_stack = None

    def barrier(self):
        toks = []
        for e in ("pe", "act", "dve", "pool"):
            if self.cnt[e]:
                toks.append(("e", e, self.cnt[e]))
        for ent in self.dma_sems:
            if ent[2] is not None:
                toks.append(ent[2])
        for e in ENGS:
            self.pending[e].extend(self._waits_for(e, toks))

    def _tstack(self):
        return self.phase_stack if self.phase_stack is not None else self.stack

    def sb(self, shape, dt=F32, name=None):
        self.n_tiles += 1
        name = (name or "t") + f"_{self.phase_id}_{self.n_tiles}"
        h = self._tstack().enter_context(self.nc.sbuf_tensor("s_" + name, list(shape), dt))
        return Tile(h, name, "sbuf")

    def ps(self, shape, dt=F32, name=None):
        self.n_tiles += 1
        name = (name or "p") + f"_{self.phase_id}_{self.n_tiles}"
        h = self._tstack().enter_context(self.nc.psum_tensor("q_" + name, list(shape), dt))
        return Tile(h, name, "psum")

    def dram(self, name, shape, dt=F32, kind="Internal"):
        h = self.nc.dram_tensor(name, list(shape), dt, kind=kind).ap()
        return Tile(h, name, "dram")

    def _deps(self, reads, writes):
        toks = []
        for v in reads:
            t = v.tile if isinstance(v, View) else v
            if t.last_w is not None:
                toks.append(t.last_w)
        for v in writes:
            t = v.tile if isinstance(v, View) else v
            if t.last_w is not None:
                toks.append(t.last_w)
            toks.extend(t.readers)
        return toks

    def _commit(self, tok, reads, writes):
        for v in reads:
            t = v.tile if isinstance(v, View) else v
            t.readers.append(tok)
        for v in writes:
            t = v.tile if isinstance(v, View) else v
            t.last_w = tok
            t.readers = []

    def _waits_for(self, eng, toks, same_engine_pe=False):
        need = {}
        for tok in toks:
            key = (tok[0], tok[1])
            if need.get(key, 0) < tok[2]:
                need[key] = tok[2]
        out = []
        for key, val in need.items():
            if key[0] == "e" and key[1] == eng and (eng == "pe" or same_engine_pe):
                continue
            if self.waited[eng].get(key, 0) >= val:
                continue
            self.waited[eng][key] = val
            sem = self.esem[key[1]] if key[0] == "e" else self.dma_sems[key[1]][0]
            out.append((sem, val))
        return out

    def op(self, eng, fn, reads=(), writes=(), extra=()):
        toks = self._deps(reads, writes) + list(extra)
        waits = self.pending[eng] + self._waits_for(eng, toks)
        self.pending[eng] = []
        self.cnt[eng] += 1
        tok = ("e", eng, self.cnt[eng])
        self.ops[eng].append((waits, fn, (self.esem[eng], 1)))
        self._commit(tok, reads, writes)
        return tok

    def dma(self, out, in_, queue="sp", is_output=False, **kw):
        reads, writes = [in_], [out]
        toks = self._deps(reads, writes)
        i = self.dma_rr
        self.dma_rr = (self.dma_rr + 1) % N_DMA_SEMS
        ent = self.dma_sems[i]
        if ent[2] is not None:
            toks.append(ent[2])
        waits = self.pending[queue] + self._waits_for(queue, toks)
        self.pending[queue] = []
        ent[1] += 16
        tok = ("d", i, ent[1])
        ent[2] = tok
        oap, iap = out.ap, in_.ap

        def fn(e, oap=oap, iap=iap, kw=kw):
            return e.dma_start(out=oap, in_=iap, **kw)

        self.ops[queue].append((waits, fn, (ent[0], 16)))
        self._commit(tok, reads, writes)
        if is_output:
            self.out_tokens.append(tok)
        return tok

    def mm(self, out, lhsT, rhs, start=True, stop=True, **kw):
        def fn(e):
            return e.matmul(out.ap, lhsT.ap, rhs.ap, start=start, stop=stop, **kw)
        return self.op("pe", fn, reads=[lhsT, rhs], writes=[out])

    def transpose(self, out, in_, ident):
        def fn(e):
            return e.transpose(out.ap, in_.ap, ident.ap)
        return self.op("pe", fn, reads=[in_, ident], writes=[out])

    def act(self, out, in_, func, bias=None, scale=1.0, accum_out=None, eng="act"):
        reads = [in_]
        kw = {}
        if isinstance(bias, View):
            reads.append(bias); kw["bias"] = bias.ap
        elif bias is not None:
            kw["bias"] = bias
        if isinstance(scale, View):
            reads.append(scale); kw["scale"] = scale.ap
        else:
            kw["scale"] = scale
        writes = [out]
        if accum_out is not None:
            writes.append(accum_out); kw["accum_out"] = accum_out.ap

        def fn(e):
            return e.activation(out.ap, in_.ap, func, **kw)
        return self.op("act", fn, reads=reads, writes=writes)

    def tt(self, out, in0, in1, op, eng="dve"):
        def fn(e):
            return e.tensor_tensor(out.ap, in0.ap, in1.ap, op)
        return self.op(eng, fn, reads=[in0, in1], writes=[out])

    def ts(self, out, in0, s1, op0, s2=None, op1=None, eng="dve", accum_out=None):
        reads = [in0]
        a1 = s1.ap if isinstance(s1, View) else s1
        a2 = s2.ap if isinstance(s2, View) else s2
        if isinstance(s1, View): reads.append(s1)
        if isinstance(s2, View): reads.append(s2)
        writes = [out]
        kw = {}
        if accum_out is not None:
            writes.append(accum_out); kw["accum_out"] = accum_out.ap

        def fn(e):
            if op1 is None:
                return e.tensor_scalar(out.ap, in0.ap, a1, None, op0, **kw)
            return e.tensor_scalar(out.ap, in0.ap, a1, a2, op0, op1, **kw)
        return self.op(eng, fn, reads=reads, writes=writes)

    def stt(self, out, in0, scalar, in1, op0, op1, eng="dve"):
        reads = [in0, in1]
        a = scalar.ap if isinstance(scalar, View) else scalar
        if isinstance(scalar, View): reads.append(scalar)

        def fn(e):
            return e.scalar_tensor_tensor(out.ap, in0.ap, a, in1.ap, op0, op1)
        return self.op("dve", fn, reads=reads, writes=[out])

    def copy(self, out, in_, eng="dve"):
        if eng == "act":
            def fn(e):
                return e.copy(out.ap, in_.ap)
        else:
            def fn(e):
                return e.tensor_copy(out.ap, in_.ap)
        return self.op(eng, fn, reads=[in_], writes=[out])

    def memset(self, out, val, eng="dve"):
        def fn(e):
            return e.memset(out.ap, val)
        return self.op(eng, fn, reads=[], writes=[out])

    def recip(self, out, in_):
        def fn(e):
            return e.reciprocal(out.ap, in_.ap)
        return self.op("dve", fn, reads=[in_], writes=[out])

    def reduce(self, out, in_, op, axis=AX.X, eng="dve"):
        def fn(e):
            return e.tensor_reduce(out.ap, in_.ap, axis, op)
        return self.op(eng, fn, reads=[in_], writes=[out])

    def finish(self):
        nc = self.nc
        fin = list(self.out_tokens)
        for e in ("pe", "act", "dve", "pool"):
            if self.cnt[e]:
                fin.append(("e", e, self.cnt[e]))
        for i, ent in enumerate(self.dma_sems):
            if ent[2] is not None:
                fin.append(ent[2])
        fwaits = self.pending["sp"] + self._waits_for("sp", fin)
        engmap = {"pe": "tensor", "act": "scalar", "dve": "vector", "pool": "gpsimd", "sp": "sync"}
        with nc.Block() as block:
            for e in ENGS:
                ops = self.ops[e]
                last = fwaits if e == "sp" else []
                if not ops and not last:
                    continue

                def body(eng, ops=ops, last=last):
                    for waits, fn, inc in ops:
                        for sem, val in waits:
                            eng.wait_ge(sem, val)
                        ins = fn(eng)
                        ins.then_inc(inc[0], inc[1])
                    for sem, val in last:
                        eng.wait_ge(sem, val)
                getattr(block, engmap[e])(body)
        self.stack.close()
        return nc


NCORES = 8
D = 2048
SEQ = 8192
LC = 256
TOK = SEQ // NCORES
DEPTH = 4
EPS = 1e-6
KC = D // 128


def run(prog, in_maps):
    nc = prog.finish()
    res = run_bass_kernel_spmd(nc, in_maps, core_ids=list(range(NCORES)))
    return res.results


def build_mods():
    P = Prog("mods"); P.setup()
    NCOL = 3 * D // NCORES
    cc = P.dram("cc", [128, KC * 2], F32, kind="ExternalInput")
    w = P.dram("w", [DEPTH, D, NCOL], F32, kind="ExternalInput")
    b = P.dram("b", [DEPTH, 2, NCOL], F32, kind="ExternalInput")
    o = P.dram("mod", [DEPTH, 2, NCOL], F32, kind="ExternalOutput")
    cc_sb = P.sb([128, KC * 2], F32, "cc_sb")
    sc = P.sb([128, KC * 2], F32, "sc")
    P.dma(cc_sb.v(), cc.v())
    P.act(sc.v(), cc_sb.v(), AF.Silu)
    wt = [P.sb([128, KC, NCOL], F32, f"wt{i}") for i in range(2)]
    bt = [P.sb([2, NCOL], F32, f"bt{i}") for i in range(2)]
    ot = [P.sb([2, NCOL], F32, f"ot{i}") for i in range(2)]
    pp = [P.ps([2, 512], F32, f"pp{i}") for i in range(4)]
    for i in range(DEPTH):
        wb = wt[i % 2]
        for j0 in range(0, KC, 4):
            P.dma(wb[:, j0:j0 + 4, :], w[i, j0 * 128:(j0 + 4) * 128, :].rearrange("(j p) n -> p j n", p=128),
                  queue="sp")
        P.dma(bt[i % 2].v(), b[i])
        for ci, (c0, c1) in enumerate(((0, 512), (512, NCOL))):
            ps = pp[(2 * i + ci) % 4]
            for j in range(KC):
                P.mm(ps[:, 0:c1 - c0], sc[:, 2 * j:2 * j + 2], wb[:, j, c0:c1], start=(j == 0), stop=(j == KC - 1))
            P.tt(ot[i % 2][:, c0:c1], ps[:, 0:c1 - c0], bt[i % 2][:, c0:c1], ALU.add)
        P.dma(o[i], ot[i % 2].v(), is_output=True)
    return P


def run_mods(inp):
    c, c_ctx, ada_w, ada_b = inp["c"], inp["c_ctx"], inp["ada_w"], inp["ada_b"]
    NCOL = 3 * D // NCORES
    cc = np.stack([c.reshape(D), c_ctx.reshape(D)], 0)
    cc_l = np.ascontiguousarray(cc.reshape(2, KC, 128).transpose(2, 1, 0).reshape(128, KC * 2))
    in_maps = []
    for r in range(NCORES):
        sl = slice(r * NCOL, (r + 1) * NCOL)
        in_maps.append({"cc": cc_l,
                        "w": np.ascontiguousarray(ada_w[:, :, sl]),
                        "b": np.ascontiguousarray(np.broadcast_to(ada_b[:, None, sl], (DEPTH, 2, NCOL)))})
    res = run(build_mods(), in_maps)
    mod = np.concatenate([res[r]["mod"] for r in range(NCORES)], axis=2)
    return mod


class Rot:
    def __init__(self, tiles):
        self.tiles = tiles
        self.i = 0

    def next(self):
        t = self.tiles[self.i % len(self.tiles)]
        self.i += 1
        return t


def rot_sb(P, n, shape, dt, name):
    return Rot([P.sb(shape, dt, f"{name}{i}") for i in range(n)])


def rot_ps(P, n, shape, dt, name):
    return Rot([P.ps(shape, dt, f"{name}{i}") for i in range(n)])


NLOC = TOK + LC
TT = [(0, 512), (512, 512), (1024, 256)]

POOL_W = 512; Q_RANK = 512; KV_RANK = 512; ROPE = 64; MLA_H = 12; MLA_NOPE = 128; MLA_V = 128
EV_IN = 3648
NKEY = SEQ + LC
NKC = NKEY // 128


class WStream:
    def __init__(self, P, stage_elems=4096, nstage=2, nbf=2):
        self.P = P
        self.n = stage_elems
        self.stage = rot_sb(P, nstage, [128, stage_elems], F32, "wstg")
        self.bf = rot_sb(P, nbf, [128, stage_elems], BF16, "wbf")
        self.k = 0

    def load(self, src_view, kch, ncols, rows=128):
        P = self.P
        assert kch * ncols <= self.n
        st = self.stage.next()
        bf = self.bf.next()
        sv = st[0:rows, 0:kch * ncols].rearrange("p (j n) -> p j n", j=kch)
        bv = bf[0:rows, 0:kch * ncols].rearrange("p (j n) -> p j n", j=kch)
        half = max(1, kch // 2)
        P.dma(sv[:, 0:half, :], src_view[:, 0:half, :], queue="sp")
        if half < kch:
            P.dma(sv[:, half:kch, :], src_view[:, half:kch, :], queue="act")
        eng = "dve" if self.k % 2 == 0 else "pool"
        self.k += 1
        P.copy(bf[0:rows, 0:kch * ncols], st[0:rows, 0:kch * ncols], eng=eng)
        return bv


def emit_hT(P, x_dram, modT, hT, ident, ps_tr):
    gcol, scol = modT
    xt = rot_sb(P, 2, [128, D], F32, "xt")
    xn = rot_sb(P, 2, [128, D], F32, "xn")
    junk = P.sb([128, D], BF16, "sqjunk")
    st = rot_sb(P, 4, [128, 4], F32, "nstat")
    for t in range(NLOC // 128):
        s = 0 if t < TOK // 128 else 1
        x = xt.next(); n_ = xn.next(); ss = st.next()
        P.dma(x.v(), x_dram[t * 128:(t + 1) * 128, :], queue="sp" if t % 2 == 0 else "act")
        P.act(junk.v(), x.v(), AF.Square, accum_out=ss[:, 0:1])
        P.ts(ss[:, 1:2], ss[:, 0:1], 1.0 / D, ALU.mult, EPS, ALU.add)
        P.act(ss[:, 2:3], ss[:, 1:2], AF.Sqrt)
        P.recip(ss[:, 3:4], ss[:, 2:3])
        P.act(n_.v(), x.v(), AF.Copy, scale=ss[:, 3:4])
        for j4 in range(KC // 4):
            pt = ps_tr.next()
            for jj in range(4):
                j = j4 * 4 + jj
                P.transpose(pt[:, jj * 128:(jj + 1) * 128], n_[:, j * 128:(j + 1) * 128], ident.v())
            for jj in range(4):
                j = j4 * 4 + jj
                if jj % 2 == 0:
                    P.ts(hT[:, j, t * 128:(t + 1) * 128], pt[:, jj * 128:(jj + 1) * 128],
                         gcol[:, s, j:j + 1], ALU.mult, scol[:, s, j:j + 1], ALU.add)
                else:
                    P.act(hT[:, j, t * 128:(t + 1) * 128], pt[:, jj * 128:(jj + 1) * 128], AF.Identity,
                          bias=scol[:, s, j:j + 1], scale=gcol[:, s, j:j + 1])


def load_mod_cols(P, modT_dram, normT_dram):
    m = P.sb([128, 2, 3, KC], F32, "modT")
    nw = P.sb([128, KC], F32, "normT")
    g = P.sb([128, 2, KC], F32, "gcol")
    P.dma(m.v().rearrange("p a b c -> p (a b c)"), modT_dram.v())
    P.dma(nw.v(), normT_dram.v())
    for s in range(2):
        P.stt(g[:, s, :], m[:, s, 1, :], 1.0, nw.v(), ALU.add, ALU.mult)
    return g, m[:, :, 0, :], m[:, :, 2, :]


def emit_rope(P, out_bf, src_f32, n, tok0, cosT, sinT, rmT, ps_rot, tmp):
    pr = ps_rot.next()
    P.mm(pr[0:64, 0:n], rmT.v(), src_f32)
    t1 = tmp.next(); t2 = tmp.next()
    P.tt(t1[0:64, 0:n], src_f32, cosT[:, tok0:tok0 + n], ALU.mult, eng="pool")
    P.tt(t2[0:64, 0:n], pr[0:64, 0:n], sinT[:, tok0:tok0 + n], ALU.mult)
    P.tt(out_bf, t1[0:64, 0:n], t2[0:64, 0:n], ALU.add)


def phase_e1(P, io):
    x_in = io["x"]
    ident_d = io["ident"]
    ident = P.sb([128, 128], F32, "ident")
    P.dma(ident.v(), ident_d.v())
    ones_bf = P.sb([128, 128], BF16, "ones_bf")
    P.memset(ones_bf.v(), 1.0)
    epsc = P.sb([128, 1], F32, "epsc")
    P.memset(epsc.v(), EPS)
    cosT = P.sb([64, TOK], F32, "cosT"); sinT = P.sb([64, TOK], F32, "sinT"); rmT = P.sb([64, 64], F32, "rmT")
    P.dma(cosT.v(), io["cosT"].v()); P.dma(sinT.v(), io["sinT"].v(), queue="act"); P.dma(rmT.v(), io["rmT"].v())
    qn_c = P.sb([128, 4], F32, "qn_c"); kvn_c = P.sb([128, 4], F32, "kvn_c")
    P.dma(qn_c.v(), io["qnormT"].v()); P.dma(kvn_c.v(), io["kvnormT"].v())
    gcol, scol, _ = load_mod_cols(P, io["modT"], io["normT"])
    hT = P.sb([128, KC, NLOC], BF16, "hT")
    ps_tr = rot_ps(P, 2, [128, 512], F32, "ps_tr")
    emit_hT(P, x_in, (gcol, scol), hT, ident, ps_tr)

    ws = WStream(P)
    ps_mm = rot_ps(P, 3, [128, 512], F32, "ps_mm")
    ps_rot = rot_ps(P, 1, [64, 512], F32, "ps_rot")
    ps_n = rot_ps(P, 1, [128, 512], F32, "ps_n")
    ostg_f = rot_sb(P, 2, [128, NLOC], F32, "ostg_f")
    ostg_b = rot_sb(P, 3, [128, NLOC], BF16, "ostg_b")
    rtmp = rot_sb(P, 4, [64, 512], F32, "rtmp")
    c_sb = P.sb([128, 4, NLOC], F32, "c_sb")
    cn_bf = P.sb([128, 4, NLOC], BF16, "cn_bf")
    sq_bf = P.sb([128, 4, 512], BF16, "sq_bf")
    rstd = P.sb([128, 512], F32, "rstd")
    w_in = io["w_in"]

    def w_in_view(c0, nc_):
        return w_in[:, c0:c0 + nc_].rearrange("(j p) n -> p j n", p=128)

    def proj_chunk(wbf, cl, m, evac):
        for ti, (t0, n) in enumerate(TT):
            ps = ps_mm.next()
            for j in range(KC):
                P.mm(ps[0:m, 0:n], wbf[:, j, cl:cl + m], hT[:, j, t0:t0 + n], start=(j == 0), stop=(j == KC - 1))
            evac(ps[0:m, 0:n], ti, t0, n)

    qdma = ["sp", "act"]
    for gi in range(4):
        wbf = ws.load(w_in_view(gi * 256, 256), KC, 256)
        for c in range(2):
            ch = gi * 2 + c
            if ch < 4:
                stg = ostg_f.next()
                proj_chunk(wbf, c * 128, 128, lambda ps, ti, t0, n, stg=stg: P.copy(stg[:, t0:t0 + n], ps, eng="dve"))
                P.dma(io["pin"][ch], stg.v(), queue=qdma[ch % 2])
            else:
                stg = ostg_f.next()
                proj_chunk(wbf, c * 128, 128, lambda ps, ti, t0, n, stg=stg: P.act(stg[:, t0:t0 + n], ps, AF.Silu))
                P.dma(io["pgs"][ch - 4], stg.v(), queue=qdma[ch % 2])
    for gi in range(6):
        wbf = ws.load(w_in_view(2112 + gi * 256, 256), KC, 256)
        for c in range(2):
            ch = gi * 2 + c
            stg = ostg_b.next()
            proj_chunk(wbf, c * 128, 128, lambda ps, ti, t0, n, stg=stg: P.act(stg[:, t0:t0 + n], ps, AF.Silu))
            P.dma(io["ags"][ch], stg.v(), queue=qdma[ch % 2])
    wbf = ws.load(w_in_view(2048, 64), KC, 64)
    stg = ostg_b.next()
    krf = P.sb([64, NLOC], F32, "krf")

    def ev_kr(ps, ti, t0, n):
        P.copy(krf[:, t0:t0 + n], ps, eng="dve")
        if t0 < TOK:
            emit_rope(P, stg[0:64, t0:t0 + n], krf[:, t0:t0 + n], n, t0, cosT, sinT, rmT, ps_rot, rtmp)
        else:
            P.copy(stg[0:64, t0:t0 + n], krf[:, t0:t0 + n], eng="pool")
    proj_chunk(wbf, 0, 64, ev_kr)
    P.dma(io["kr"].v(), stg[0:64, :])

    for which in range(2):
        cbase = 1024 + which * 512
        for gi in range(2):
            wbf = ws.load(w_in_view(cbase + gi * 256, 256), KC, 256)
            for c in range(2):
                ch = gi * 2 + c
                proj_chunk(wbf, c * 128, 128, lambda ps, ti, t0, n, ch=ch: P.copy(c_sb[:, ch, t0:t0 + n], ps, eng="dve"))
        ncol = qn_c if which == 0 else kvn_c
        for ti, (t0, n) in enumerate(TT):
            for j in range(4):
                P.act(sq_bf[:, j, 0:n], c_sb[:, j, t0:t0 + n], AF.Square)
            pn = ps_n.next()
            for j in range(4):
                P.mm(pn[:, 0:n], ones_bf.v(), sq_bf[:, j, 0:n], start=(j == 0), stop=(j == 3))
            P.act(rstd[:, 0:n], pn[:, 0:n], AF.Sqrt, bias=epsc[:, 0:1], scale=1.0 / 512)
            P.recip(rstd[:, 0:n], rstd[:, 0:n])
            for j in range(4):
                P.stt(cn_bf[:, j, t0:t0 + n], c_sb[:, j, t0:t0 + n], ncol[:, j:j + 1], rstd[:, 0:n], ALU.mult, ALU.mult,
                      eng="dve" if j % 2 == 0 else "pool")
        if which == 0:
            w_uq = io["w_uq"]
            for hg in range(3):
                wbf = ws.load(w_uq[:, hg * 768:(hg + 1) * 768].rearrange("(j p) n -> p j n", p=128), 4, 768)
                for hh in range(4):
                    h = hg * 4 + hh
                    stg = ostg_b.next()
                    for ti, (t0, n) in enumerate(TT):
                        ps = ps_mm.next()
                        for j in range(4):
                            P.mm(ps[:, 0:n], wbf[:, j, hh * 192:hh * 192 + 128], cn_bf[:, j, t0:t0 + n], start=(j == 0), stop=(j == 3))
                        P.copy(stg[:, t0:t0 + n], ps[:, 0:n], eng="act")
                    P.dma(io["qn"][h], stg.v(), queue=qdma[h % 2])
                    stg = ostg_b.next()
                    for ti, (t0, n) in enumerate(TT):
                        ps = ps_mm.next()
                        for j in range(4):
                            P.mm(ps[0:64, 0:n], wbf[:, j, hh * 192 + 128:hh * 192 + 192], cn_bf[:, j, t0:t0 + n], start=(j == 0), stop=(j == 3))
                        if t0 < TOK:
                            rf = rtmp.next()
                            P.copy(rf[0:64, 0:n], ps[0:64, 0:n], eng="act")
                            emit_rope(P, stg[0:64, t0:t0 + n], rf[0:64, 0:n], n, t0, cosT, sinT, rmT, ps_rot, rtmp)
                        else:
                            P.copy(stg[0:64, t0:t0 + n], ps[0:64, 0:n], eng="act")
                    P.dma(io["qr"][h], stg[0:64, :], queue=qdma[(h + 1) % 2])
        else:
            w_ukv = io["w_ukv"]
            wv5 = w_ukv.v().rearrange("(j p) (h two c) -> p j h two c", p=128, two=2, c=128)
            for hg in range(2):
                wbf = None
                st_ = ws.stage.next(); bf_ = ws.bf.next()
                sv = st_[:, 0:4 * 768].rearrange("p (j h c) -> p j h c", j=4, h=6)
                for j in range(4):
                    P.dma(sv[:, j], wv5[:, j, hg * 6:(hg + 1) * 6, 0, :], queue=qdma[j % 2])
                P.copy(bf_[:, 0:4 * 768], st_[:, 0:4 * 768], eng="dve")
                wbf = bf_[:, 0:4 * 768].rearrange("p (j n) -> p j n", j=4)
                for hh in range(6):
                    h = hg * 6 + hh
                    stg = ostg_b.next()
                    for ti, (t0, n) in enumerate(TT):
                        ps = ps_mm.next()
                        for j in range(4):
                            P.mm(ps[:, 0:n], wbf[:, j, hh * 128:(hh + 1) * 128], cn_bf[:, j, t0:t0 + n], start=(j == 0), stop=(j == 3))
                        P.copy(stg[:, t0:t0 + n], ps[:, 0:n], eng="act")
                    P.dma(io["kn"][h], stg.v(), queue=qdma[h % 2])
            vst = rot_sb(P, 2, [128, 1536], BF16, "vst")
            wvb = []
            for hg in range(2):
                st_ = ws.stage.next(); bf_ = ws.bf.next()
                sv = st_[:, 0:4 * 768].rearrange("p (j h c) -> p j h c", j=4, h=6)
                for j in range(4):
                    P.dma(sv[:, j], wv5[:, j, hg * 6:(hg + 1) * 6, 1, :], queue=qdma[j % 2])
                P.copy(bf_[:, 0:4 * 768], st_[:, 0:4 * 768], eng="pool")
                wvb.append(bf_[:, 0:4 * 768].rearrange("p (j n) -> p j n", j=4))
            for t in range(NLOC // 128):
                vs = vst.next()
                for hg in range(2):
                    for half in range(2):
                        ps = ps_mm.next()
                        for j in range(4):
                            P.mm(ps[:, 0:384], cn_bf[:, j, t * 128:(t + 1) * 128], wvb[hg][:, j, half * 384:(half + 1) * 384],
                                 start=(j == 0), stop=(j == 3))
                        P.copy(vs[:, hg * 768 + half * 384:hg * 768 + (half + 1) * 384], ps[:, 0:384], eng="act" if half else "dve")
                P.dma(io["v"][t * 128:(t + 1) * 128, :], vs.v(), queue=qdma[t % 2])


def rope_tables():
    t = np.arange(SEQ)
    row = (t // 64).astype(np.float32); col = (t % 64).astype(np.float32)
    inv = (np.float32(10000.0) ** (-np.arange(0, 32, 2, dtype=np.float32) / np.float32(32))).astype(np.float32)
    ar = row[None, :] * inv[:, None]; ac = col[None, :] * inv[:, None]
    ang = np.concatenate([ar, ar, ac, ac], 0)
    cosT = np.cos(ang).astype(np.float32); sinT = np.sin(ang).astype(np.float32)
    rm = np.zeros((64, 64), np.float32)
    for b in (0, 32):
        for i in range(16):
            rm[b + i, b + i + 16] = -1.0
            rm[b + i + 16, b + i] = 1.0
    return cosT, sinT, np.ascontiguousarray(rm.T)


def colT(v, kc):
    return np.ascontiguousarray(np.asarray(v, np.float32).reshape(kc, 128).T)


def mod_cols(mod_i):
    m = mod_i.reshape(2, 3, KC, 128).transpose(3, 0, 1, 2).reshape(128, 2 * 3 * KC)
    return np.ascontiguousarray(m)


def mk_io(P, spec, ext_in, ext_out):
    io = {}
    for name, (shape, dt) in spec.items():
        kind = "ExternalInput" if name in ext_in else ("ExternalOutput" if name in ext_out else "Internal")
        io[name] = P.dram(name, shape, dt, kind=kind)
    return io


E1_IN = {"x": ([NLOC, D], F32), "ident": ([128, 128], F32), "cosT": ([64, TOK], F32), "sinT": ([64, TOK], F32),
         "rmT": ([64, 64], F32), "qnormT": ([128, 4], F32), "kvnormT": ([128, 4], F32),
         "modT": ([128, 2 * 3 * KC], F32), "normT": ([128, KC], F32),
         "w_in": ([D, EV_IN], F32), "w_uq": ([512, 2304], F32), "w_ukv": ([512, 3072], F32)}
E1_OUT = {"pin": ([4, 128, NLOC], F32), "pgs": ([4, 128, NLOC], F32), "ags": ([12, 128, NLOC], BF16),
          "kr": ([64, NLOC], BF16), "qn": ([12, 128, NLOC], BF16), "qr": ([12, 64, NLOC], BF16),
          "kn": ([12, 128, NLOC], BF16), "v": ([NLOC, 1536], BF16)}


def build_e1():
    P = Prog("e1"); P.setup()
    io = mk_io(P, {**E1_IN, **E1_OUT}, set(E1_IN), set(E1_OUT))
    phase_e1(P, io)
    return P


def e1_inputs(r, x, xc, mod_i, norm_w_i, w_in, q_norm, kv_norm, w_uq, w_ukv, tabs):
    cosT, sinT, rmT = tabs
    xl = np.concatenate([x[r * TOK:(r + 1) * TOK], xc], 0)
    return {"x": np.ascontiguousarray(xl), "ident": np.eye(128, dtype=np.float32),
            "cosT": np.ascontiguousarray(cosT[:, r * TOK:(r + 1) * TOK]),
            "sinT": np.ascontiguousarray(sinT[:, r * TOK:(r + 1) * TOK]), "rmT": rmT,
            "qnormT": colT(q_norm, 4), "kvnormT": colT(kv_norm, 4), "modT": mod_cols(mod_i), "normT": colT(norm_w_i, KC),
            "w_in": w_in, "w_uq": w_uq, "w_ukv": w_ukv}


E2A_IN = {"qn": ([12, 128, NLOC], BF16), "qr": ([12, 64, NLOC], BF16), "ags": ([12, 128, NLOC], BF16),
          "pgs": ([4, 128, NLOC], F32), "pinx": ([4, 128, TOK + 16], F32), "pinc": ([4, 128, LC + 16], F32),
          "rcl": ([128, 4, TOK], F32), "rcc": ([128, 4, LC], F32),
          "KN": ([12, 128, NKEY], BF16), "KR": ([64, NKEY], BF16), "V": ([NKEY, 1536], BF16),
          "w_pool": ([4, 128, 128], F32), "pscaleT": ([128, 4], F32)}
E2A_OUT = {"mixT": ([16, 128, NLOC], BF16)}


def phase_e2a(P, io):
    ones_bf = P.sb([128, 128], BF16, "ones_bf")
    P.memset(ones_bf.v(), 1.0)
    ps_s = rot_ps(P, 3, [128, 512], F32, "ps_s")
    ps_o = rot_ps(P, 2, [128, 512], F32, "ps_o")
    ps_d = rot_ps(P, 2, [128, 512], F32, "ps_d")
    ps_p = rot_ps(P, 1, [128, 512], F32, "ps_p")
    mstg = rot_sb(P, 3, [128, NLOC], BF16, "mstg")
    qd = ["sp", "act"]
    wp_f = P.sb([128, 4, 128], F32, "wp_f"); wp_b = P.sb([128, 4, 128], BF16, "wp_b")
    P.dma(wp_f.v(), io["w_pool"].v().rearrange("g c d -> c g d"))
    P.copy(wp_b.v(), wp_f.v(), eng="pool")
    psc = P.sb([128, 4], F32, "psc"); P.dma(psc.v(), io["pscaleT"].v())
    rcl = P.sb([128, 4, TOK], F32, "rcl"); rcc = P.sb([128, 4, LC], F32, "rcc")
    P.dma(rcl.v(), io["rcl"].v()); P.dma(rcc.v(), io["rcc"].v(), queue="act")
    U = rot_sb(P, 2, [128, TOK + 16], F32, "poolU")
    A = rot_sb(P, 2, [128, TOK + 16], F32, "poolA")
    zb = P.sb([128, NLOC], BF16, "poolz")
    pg = rot_sb(P, 2, [128, NLOC], F32, "poolpg")
    for g in range(4):
        w = 2 << g
        for (src, n, rc, t0) in ((io["pinx"], TOK, rcl, 0), (io["pinc"], LC, rcc, TOK)):
            u = U.next()
            P.dma(u[:, 0:n + 16], src[g], queue=qd[g % 2])
            cur = u; L = n + 16; step = 1
            while step < w:
                nxt = A.next()
                L2 = L - step
                P.tt(nxt[:, 0:L2], cur[:, 0:L2], cur[:, step:step + L2], ALU.add, eng="pool")
                cur = nxt; L = L2; step *= 2
            off = 8 - w // 2
            t1 = A.next()
            P.tt(t1[:, 0:n], cur[:, off:off + n], rc[:, g, :], ALU.mult)
            P.tt(zb[:, t0:t0 + n], t1[:, 0:n], u[:, 8:8 + n], ALU.subtract)
        pgt = pg.next()
        P.dma(pgt.v(), io["pgs"][g], queue=qd[(g + 1) % 2])
        stg = mstg.next()
        for (t0, n) in TT:
            ps = ps_p.next()
            P.mm(ps[:, 0:n], wp_b[:, g, :], zb[:, t0:t0 + n])
            P.stt(stg[:, t0:t0 + n], ps[:, 0:n], psc[:, g:g + 1], pgt[:, t0:t0 + n], ALU.mult, ALU.mult)
        P.dma(io["mixT"][g], stg.v(), queue=qd[g % 2])
    kr = P.sb([64, NKEY], BF16, "kr_all")
    P.dma(kr[:, 0:NKEY // 2], io["KR"][:, 0:NKEY // 2]); P.dma(kr[:, NKEY // 2:], io["KR"][:, NKEY // 2:], queue="act")
    knb = rot_sb(P, 2, [128, NKEY], BF16, "knb")
    vb = rot_sb(P, 2, [128, NKC, 128], BF16, "vb")
    qnb = rot_sb(P, 2, [128, NLOC], BF16, "qnb")
    qrb = rot_sb(P, 2, [64, NLOC], BF16, "qrb")
    agb = rot_sb(P, 2, [128, NLOC], BF16, "agb")
    pT = rot_sb(P, 3, [128, 512], BF16, "pT")
    rden = rot_sb(P, 2, [128, 512], F32, "rden")
    otmp = rot_sb(P, 2, [128, 512], F32, "otmp")
    scale = float(192 ** -0.5)
    for h in range(MLA_H):
        kn = knb.next(); v = vb.next(); qn = qnb.next(); qr = qrb.next(); ag = agb.next()
        hk = NKEY // 2
        P.dma(kn[:, 0:hk], io["KN"][h, :, 0:hk], queue="sp"); P.dma(kn[:, hk:], io["KN"][h, :, hk:], queue="act")
        vsrc = io["V"][:, h * 128:(h + 1) * 128].rearrange("(c p) d -> p c d", p=128)
        P.dma(v[:, 0:NKC // 2, :], vsrc[:, 0:NKC // 2, :], queue="sp"); P.dma(v[:, NKC // 2:, :], vsrc[:, NKC // 2:, :], queue="act")
        P.dma(qn.v(), io["qn"][h], queue="sp"); P.dma(qr.v(), io["qr"][h], queue="act"); P.dma(ag.v(), io["ags"][h], queue="sp")
        stg = mstg.next()
        for (t0, n) in TT:
            chunks = range(NKC) if t0 < TOK else range(SEQ // 128, NKC)
            po = ps_o.next(); pd = ps_d.next()
            first = True
            nch = len(chunks)
            for ci, c in enumerate(chunks):
                ps = ps_s.next()
                P.mm(ps[:, 0:n], kn[:, c * 128:(c + 1) * 128], qn[:, t0:t0 + n], start=True, stop=False)
                P.mm(ps[:, 0:n], kr[:, c * 128:(c + 1) * 128], qr[:, t0:t0 + n], start=False, stop=True)
                p = pT.next()
                P.act(p[:, 0:n], ps[:, 0:n], AF.Exp, scale=scale)
                P.mm(po[:, 0:n], v[:, c, :], p[:, 0:n], start=(ci == 0), stop=(ci == nch - 1))
                P.mm(pd[:, 0:n], ones_bf.v(), p[:, 0:n], start=(ci == 0), stop=(ci == nch - 1))
            rd = rden.next(); ot = otmp.next()
            P.recip(rd[:, 0:n], pd[:, 0:n])
            P.tt(ot[:, 0:n], po[:, 0:n], rd[:, 0:n], ALU.mult)
            P.tt(stg[:, t0:t0 + n], ot[:, 0:n], ag[:, t0:t0 + n], ALU.mult, eng="pool")
        P.dma(io["mixT"][4 + h], stg.v(), queue=qd[h % 2])


def build_e2a():
    P = Prog("e2a"); P.setup()
    io = mk_io(P, {**E2A_IN, **E2A_OUT}, set(E2A_IN), set(E2A_OUT))
    phase_e2a(P, io)
    return P


POOL_WINDOWS = (2, 4, 8, 16)


def pool_rc(L):
    t = np.arange(L)
    out = np.zeros((4, L), np.float32)
    for g, w in enumerate(POOL_WINDOWS):
        lo = np.clip(t - w // 2, 0, L - 1); hi = np.clip(t + w // 2 - 1, 0, L - 1)
        out[g] = (1.0 / (hi - lo + 1).astype(np.float32)).astype(np.float32)
    return out


def e2a_inputs(r, e1o, KN, KR, V, w_pool, pool_scale):
    o = e1o[r]
    z8 = np.zeros((4, 128, 8), np.float32)
    left = e1o[r - 1]["pin"][:, :, TOK - 8:TOK] if r > 0 else z8
    right = e1o[r + 1]["pin"][:, :, 0:8] if r < NCORES - 1 else z8
    pinx = np.concatenate([left, o["pin"][:, :, 0:TOK], right], 2)
    pinc = np.concatenate([z8, o["pin"][:, :, TOK:], z8], 2)
    rcl = np.broadcast_to(pool_rc(SEQ)[None, :, r * TOK:(r + 1) * TOK], (128, 4, TOK))
    rcc = np.broadcast_to(pool_rc(LC)[None], (128, 4, LC))
    return {"qn": o["qn"], "qr": o["qr"], "ags": o["ags"], "pgs": o["pgs"], "pinx": np.ascontiguousarray(pinx),
            "pinc": np.ascontiguousarray(pinc), "rcl": np.ascontiguousarray(rcl), "rcc": np.ascontiguousarray(rcc),
            "KN": KN, "KR": KR, "V": V, "w_pool": w_pool, "pscaleT": colT(pool_scale, 4)}


def gather_kv(e1o):
    KN = np.concatenate([o["kn"][:, :, 0:TOK] for o in e1o] + [e1o[0]["kn"][:, :, TOK:]], 2)
    KR = np.concatenate([o["kr"][:, 0:TOK] for o in e1o] + [e1o[0]["kr"][:, TOK:]], 1)
    V = np.concatenate([o["v"][0:TOK] for o in e1o] + [e1o[0]["v"][TOK:]], 0)
    return np.ascontiguousarray(KN), np.ascontiguousarray(KR), np.ascontiguousarray(V)


OUT_IN = {"mixT": ([16, 128, NLOC], BF16), "w_out": ([D, D], F32), "x": ([NLOC, D], F32), "gate": ([2, D], F32)}
OUT_OUT = {"xo": ([NLOC, D], F32)}


def phase_out(P, io, final=False):
    qd = ["sp", "act"]
    mix = P.sb([128, 16, NLOC], BF16, "mix")
    for j in range(16):
        P.dma(mix[:, j, :], io["mixT"][j], queue=qd[j % 2])
    gate = P.sb([128, 2, D], F32, "gate_bc")
    for s in range(2):
        P.dma(gate[:, s, :], io["gate"][s:s + 1, :].partition_broadcast(128), queue=qd[s])
    if final:
        fnw = P.sb([128, D], F32, "fnw_bc")
        P.dma(fnw.v(), io["fnw"][0:1, :].partition_broadcast(128))
        st = rot_sb(P, 2, [128, 4], F32, "fstat")
    wstage = P.sb([128, 8192], F32, "wostage")
    wres = P.sb([128, 16, D], BF16, "wres")
    for cg in range(4):
        st_ = wstage
        sv = st_.v().rearrange("p (j n) -> p j n", j=16)
        src = io["w_out"][:, cg * 512:(cg + 1) * 512].rearrange("(j p) n -> p j n", p=128)
        P.dma(sv[:, 0:8, :], src[:, 0:8, :], queue="sp"); P.dma(sv[:, 8:16, :], src[:, 8:16, :], queue="act")
        P.copy(wres[:, :, cg * 512:(cg + 1) * 512], sv, eng="dve" if cg % 2 == 0 else "pool")
    ps_y = rot_ps(P, 4, [128, 512], F32, "ps_y")
    xt = rot_sb(P, 2, [128, D], F32, "oxt")
    tmp = rot_sb(P, 2, [128, D], F32, "otmpy")
    ntile = (TOK if final else NLOC) // 128
    for t in range(ntile):
        s = 0 if t < TOK // 128 else 1
        x = xt.next(); tm = tmp.next()
        P.dma(x.v(), io["x"][t * 128:(t + 1) * 128, :], queue=qd[t % 2])
        for cg in range(4):
            ps = ps_y.next()
            for j in range(16):
                P.mm(ps.v(), mix[:, j, t * 128:(t + 1) * 128], wres[:, j, cg * 512:(cg + 1) * 512], start=(j == 0), stop=(j == 15))
            P.tt(tm[:, cg * 512:(cg + 1) * 512], ps.v(), gate[:, s, cg * 512:(cg + 1) * 512], ALU.mult)
        P.tt(x.v(), x.v(), tm.v(), ALU.add, eng="pool")
        if not final:
            P.dma(io["xo"][t * 128:(t + 1) * 128, :], x.v(), queue=qd[(t + 1) % 2], is_output=True)
        else:
            ss = st.next(); o = tm
            P.act(tm.v(), x.v(), AF.Square, accum_out=ss[:, 0:1])
            P.ts(ss[:, 1:2], ss[:, 0:1], 1.0 / D, ALU.mult, EPS, ALU.add)
            P.act(ss[:, 2:3], ss[:, 1:2], AF.Sqrt)
            P.recip(ss[:, 3:4], ss[:, 2:3])
            P.stt(o.v(), x.v(), ss[:, 3:4], fnw.v(), ALU.mult, ALU.mult)
            P.dma(io["out"][t * 128:(t + 1) * 128, :], o.v(), queue=qd[(t + 1) % 2], is_output=True)


def build_out(final=False):
    P = Prog("outf" if final else "outp"); P.setup()
    spec_in = dict(OUT_IN)
    spec_out = dict(OUT_OUT)
    if final:
        spec_in["fnw"] = ([1, D], F32)
        spec_out = {"out": ([TOK, D], F32)}
    io = mk_io(P, {**spec_in, **spec_out}, set(spec_in), set(spec_out))
    phase_out(P, io, final)
    return P


OD_IN = 6416
NT128 = NLOC // 128
O1_IN = {"x": ([NLOC, D], F32), "ident": ([128, 128], F32), "cosT": ([64, TOK], F32), "sinT": ([64, TOK], F32),
         "rmT": ([64, 64], F32), "modT": ([128, 2 * 3 * KC], F32), "normT": ([128, KC], F32),
         "w_in": ([D, OD_IN], F32), "gate_b": ([1, 16], F32)}
O1_OUT = {"mqT": ([4, 128, NLOC], BF16), "mkT": ([4, 128, NLOC], BF16), "mk": ([NLOC, 512], BF16),
          "mva": ([NLOC, 4 * 257], BF16), "og": ([NLOC, 1024], F32), "gts": ([NLOC, 16], F32),
          "aqT": ([16, 64, NLOC], BF16), "akT": ([2, 64, NLOC], BF16), "av": ([NLOC, 128], BF16),
          "azT": ([16, 64, NLOC], BF16)}


def phase_o1(P, io):
    ident = P.sb([128, 128], F32, "ident")
    P.dma(ident.v(), io["ident"].v())
    cosT = P.sb([64, TOK], F32, "cosT"); sinT = P.sb([64, TOK], F32, "sinT"); rmT = P.sb([64, 64], F32, "rmT")
    P.dma(cosT.v(), io["cosT"].v()); P.dma(sinT.v(), io["sinT"].v(), queue="act"); P.dma(rmT.v(), io["rmT"].v())
    onec = P.sb([128, 1], F32, "onec"); P.memset(onec.v(), 1.0)
    gcol, scol, _ = load_mod_cols(P, io["modT"], io["normT"])
    hT = P.sb([128, KC, NLOC], BF16, "hT")
    ps_tr = rot_ps(P, 2, [128, 512], F32, "ps_tr")
    emit_hT(P, io["x"], (gcol, scol), hT, ident, ps_tr)
    ws = WStream(P)
    ps_mm = rot_ps(P, 3, [128, 512], F32, "ps_mm")
    ps_rot = rot_ps(P, 1, [64, 512], F32, "ps_rot")
    ostg_b = rot_sb(P, 3, [128, NLOC], BF16, "ostg_b")
    rtmp = rot_sb(P, 4, [64, 512], F32, "rtmp")
    w_in = io["w_in"]
    qd = ["sp", "act"]

    def w_in_view(c0, nc_):
        return w_in[:, c0:c0 + nc_].rearrange("(j p) n -> p j n", p=128)

    def proj_fm(wbf, cl, m, evac):
        for ti, (t0, n) in enumerate(TT):
            ps = ps_mm.next()
            for j in range(KC):
                P.mm(ps[0:m, 0:n], wbf[:, j, cl:cl + m], hT[:, j, t0:t0 + n], start=(j == 0), stop=(j == KC - 1))
            evac(ps[0:m, 0:n], ti, t0, n)

    qscale = float(128 ** -0.5)
    for gi in range(4):
        wbf = ws.load(w_in_view(gi * 256, 256), KC, 256)
        for c in range(2):
            ch = gi * 2 + c
            stg = ostg_b.next()
            if ch < 4:
                proj_fm(wbf, c * 128, 128, lambda ps, ti, t0, n, stg=stg: P.act(stg[:, t0:t0 + n], ps, AF.Copy, scale=qscale))
                P.dma(io["mqT"][ch], stg.v(), queue=qd[ch % 2])
            else:
                proj_fm(wbf, c * 128, 128, lambda ps, ti, t0, n, stg=stg: P.copy(stg[:, t0:t0 + n], ps, eng="dve"))
                P.dma(io["mkT"][ch - 4], stg.v(), queue=qd[ch % 2])
    def rope_head(wbf, cl, dst):
        stg = ostg_b.next()

        def ev(ps, ti, t0, n):
            if t0 < TOK:
                rf = rtmp.next()
                P.copy(rf[0:64, 0:n], ps, eng="act")
                emit_rope(P, stg[0:64, t0:t0 + n], rf[0:64, 0:n], n, t0, cosT, sinT, rmT, ps_rot, rtmp)
            else:
                P.copy(stg[0:64, t0:t0 + n], ps, eng="act")
        proj_fm(wbf, cl, 64, ev)
        P.dma(dst, stg[0:64, :])
    for gi in range(4):
        wbf = ws.load(w_in_view(4112 + gi * 256, 256), KC, 256)
        for c in range(4):
            rope_head(wbf, c * 64, io["aqT"][gi * 4 + c])
    wbf = ws.load(w_in_view(5136, 128), KC, 128)
    for c in range(2):
        rope_head(wbf, c * 64, io["akT"][c])
    for gi in range(4):
        wbf = ws.load(w_in_view(5392 + gi * 256, 256), KC, 256)
        for c in range(4):
            stg = ostg_b.next()
            proj_fm(wbf, c * 64, 64, lambda ps, ti, t0, n, stg=stg: P.act(stg[0:64, t0:t0 + n], ps, AF.Silu))
            P.dma(io["azT"][gi * 4 + c], stg[0:64, :], queue=qd[c % 2])

    def proj_tm(wbf, ncols, evac):
        for t in range(NT128):
            ps = ps_mm.next()
            for j in range(KC):
                P.mm(ps[:, 0:ncols], hT[:, j, t * 128:(t + 1) * 128], wbf[:, j, 0:ncols], start=(j == 0), stop=(j == KC - 1))
            evac(ps[:, 0:ncols], t)
    tstg_b = rot_sb(P, 2, [128, NT128, 256], BF16, "tstg_b")
    for gi in range(2):
        wbf = ws.load(w_in_view(512 + gi * 256, 256), KC, 256)
        stg = tstg_b.next()
        proj_tm(wbf, 256, lambda ps, t, stg=stg: P.copy(stg[:, t, :], ps, eng="dve" if t % 2 == 0 else "act"))
        P.dma(io["mk"][:, gi * 256:(gi + 1) * 256].rearrange("(t p) n -> p t n", p=128), stg.v(), queue=qd[gi % 2])
    vstg = rot_sb(P, 2, [128, NT128, 257], BF16, "vstg")
    mva4 = io["mva"].v().rearrange("(t p) (h c) -> p t h c", p=128, h=4)
    for hd in range(4):
        wbf = ws.load(w_in_view(1024 + hd * 256, 256), KC, 256)
        stg = vstg.next()
        P.memset(stg[:, :, 256:257], 1.0, eng="pool")
        proj_tm(wbf, 256, lambda ps, t, stg=stg: P.copy(stg[:, t, 0:256], ps, eng="dve" if t % 2 == 0 else "act"))
        P.dma(mva4[:, :, hd, :], stg.v(), queue=qd[hd % 2])
    ogstg = rot_sb(P, 2, [128, NT128, 256], F32, "ogstg")
    sg = rot_sb(P, 2, [128, 256], F32, "sgt"); sz = rot_sb(P, 2, [128, 256], F32, "szt")
    for gi in range(4):
        wo = ws.load(w_in_view(2048 + gi * 256, 256), KC, 256)
        wz = ws.load(w_in_view(3088 + gi * 256, 256), KC, 256)
        stg = ogstg.next()
        for t in range(NT128):
            pso = ps_mm.next(); psz = ps_mm.next()
            for j in range(KC):
                P.mm(pso[:, 0:256], hT[:, j, t * 128:(t + 1) * 128], wo[:, j, :], start=(j == 0), stop=(j == KC - 1))
            for j in range(KC):
                P.mm(psz[:, 0:256], hT[:, j, t * 128:(t + 1) * 128], wz[:, j, :], start=(j == 0), stop=(j == KC - 1))
            a = sg.next(); b = sz.next()
            P.act(a.v(), pso[:, 0:256], AF.Sigmoid)
            P.act(b.v(), psz[:, 0:256], AF.Silu)
            P.tt(stg[:, t, :], a.v(), b.v(), ALU.mult)
        P.dma(io["og"][:, gi * 256:(gi + 1) * 256].rearrange("(t p) n -> p t n", p=128), stg.v(), queue=qd[gi % 2])
    wbf = ws.load(w_in_view(5264, 128), KC, 128)
    stg = tstg_b.next()
    proj_tm(wbf, 128, lambda ps, t, stg=stg: P.copy(stg[:, t, 0:128], ps, eng="dve"))
    P.dma(io["av"].v().rearrange("(t p) n -> p t n", p=128), stg[:, :, 0:128])
    wbf = ws.load(w_in_view(3072, 16), KC, 16)
    gb = P.sb([128, 16], F32, "gb_bc")
    P.dma(gb.v(), io["gate_b"][0:1, :].partition_broadcast(128))
    G = P.sb([128, NT128, 16], F32, "Gstg")
    e_ = P.sb([128, NT128, 16], F32, "Gexp")
    proj_tm(wbf, 16, lambda ps, t: P.tt(G[:, t, :], ps, gb.v(), ALU.add))
    G4 = G.v().rearrange("p t (a b c) -> p t a b c", a=2, b=2)
    E4 = e_.v().rearrange("p t (a b c) -> p t a b c", a=2, b=2)
    for t in range(NT128):
        P.act(E4[:, t, :, 1, :], G4[:, t, :, 1, :], AF.Exp, scale=-1.0)
        P.act(E4[:, t, :, 1, :], E4[:, t, :, 1, :], AF.Ln, bias=onec[:, 0:1])
        P.ts(G4[:, t, :, 1, :], E4[:, t, :, 1, :], -1.0, ALU.mult)
    P.dma(io["gts"].v().rearrange("(t p) n -> p t n", p=128), G.v())


MCONST_IN = {"tri": ([2, 128, 128], F32), "neg": ([2, 128, 128], F32)}


def mlstm_consts():
    r = np.arange(128)
    tri_f = (r[:, None] <= r[None, :]).astype(np.float32)
    tri_b = (r[:, None] >= r[None, :]).astype(np.float32)
    neg_f = np.where(r[:, None] <= r[None, :], 0.0, -30000.0).astype(np.float32)
    neg_b = np.where(r[:, None] >= r[None, :], 0.0, -30000.0).astype(np.float32)
    return {"tri": np.stack([tri_f, tri_b]), "neg": np.stack([neg_f, neg_b])}


class MEnv:
    pass


def mlstm_setup(P, io, need_q=True):
    E = MEnv()
    E.tri = P.sb([128, 2, 128], F32, "tri"); E.neg = P.sb([128, 2, 128], F32, "neg")
    P.dma(E.tri.v(), io["tri"].v().rearrange("a p t -> p a t")); P.dma(E.neg.v(), io["neg"].v().rearrange("a p t -> p a t"), queue="act")
    E.ones = P.sb([128, 128], F32, "ones_f"); P.memset(E.ones.v(), 1.0)
    E.G = P.sb([128, NT128, 16], F32, "mG")
    P.dma(E.G.v(), io["gts"].v().rearrange("(t p) n -> p t n", p=128))
    E.k = P.sb([128, NT128, 512], BF16, "mk_tok")
    P.dma(E.k.v(), io["mk"].v().rearrange("(t p) n -> p t n", p=128), queue="act")
    E.va = P.sb([128, NT128, 4 * 257], BF16, "mva")
    P.dma(E.va[:, 0:5, :], io["mva"][0:640, :].rearrange("(t p) n -> p t n", p=128))
    P.dma(E.va[:, 5:10, :], io["mva"][640:1280, :].rearrange("(t p) n -> p t n", p=128), queue="act")
    E.kT = P.sb([128, 4, NLOC], BF16, "mkT")
    P.dma(E.kT.v(), io["mkT"].v().rearrange("h p t -> p h t"))
    if need_q:
        E.qT = P.sb([128, 4, NLOC], BF16, "mqT")
        P.dma(E.qT.v(), io["mqT"].v().rearrange("h p t -> p h t"), queue="act")
    E.ps_b = rot_ps(P, 2, [128, 512], F32, "mps_b")
    E.ps_qk = rot_ps(P, 1, [128, 512], F32, "mps_qk")
    E.ps_o = rot_ps(P, 2, [128, 512], F32, "mps_o")
    E.ps_c = rot_ps(P, 2, [128, 512], F32, "mps_c")
    E.LFb = rot_sb(P, 3, [128, 128], F32, "mLFb")
    E.bbs = rot_sb(P, 3, [128, 128], F32, "mbbs")
    E.DT = rot_sb(P, 3, [128, 128], F32, "mDT")
    E.ST = rot_sb(P, 3, [128, 128], BF16, "mST")
    E.eb = rot_sb(P, 3, [128, 128], F32, "meb")
    E.qs = rot_sb(P, 3, [128, 128], BF16, "mqs")
    E.cols = rot_sb(P, 6, [128, 8], F32, "mcols")
    E.vw = rot_sb(P, 3, [128, 257], BF16, "mvw")
    E.Cf = [[P.sb([128, 257], F32, f"Cf{h}{d}") for d in range(2)] for h in range(4)]
    E.Cb = [[P.sb([128, 257], BF16, f"Cb{h}{d}") for d in range(2)] for h in range(4)]
    E.etot = P.sb([128, 8], F32, "etot")
    return E


def mlstm_step(P, E, hd, dr, ch, mode, hsum=None, written=None):
    c = E.cols.next()
    lf = E.G[:, ch, 4 + 8 * dr + hd:5 + 8 * dr + hd]
    ig = E.G[:, ch, 8 * dr + hd:8 * dr + hd + 1]
    tri = E.tri[:, dr, :]; neg = E.neg[:, dr, :]
    lfb = E.LFb.next()
    P.ts(lfb.v(), E.ones.v(), lf, ALU.mult)
    pb = E.ps_b.next()
    P.mm(pb[:, 0:128], lfb.v(), tri)
    P.mm(pb[:, 256:257], tri, lf)
    last = 127 if dr == 0 else 0
    P.tt(c[:, 0:1], ig, pb[:, 256:257], ALU.subtract)
    P.copy(c[:, 1:2], pb[:, last:last + 1], eng="dve")
    P.act(c[:, 2:3], c[:, 0:1], AF.Exp, bias=c[:, 1:2])
    P.act(c[:, 3:4], c[:, 1:2], AF.Exp)
    Cf = E.Cf[hd][dr]; Cb = E.Cb[hd][dr]
    va = E.va[:, ch, hd * 257:(hd + 1) * 257]
    if mode == "B":
        bbs = E.bbs.next()
        P.tt(bbs.v(), pb[:, 0:128], neg, ALU.add)
        dt_ = E.DT.next()
        P.act(dt_.v(), bbs.v(), AF.Exp, bias=c[:, 0:1])
        pq = E.ps_qk.next()
        P.mm(pq[:, 0:128], E.kT[:, hd, ch * 128:(ch + 1) * 128], E.qT[:, hd, ch * 128:(ch + 1) * 128])
        st = E.ST.next()
        P.tt(st.v(), pq[:, 0:128], dt_.v(), ALU.mult)
        eb = E.eb.next()
        P.act(eb.v(), pb[:, 0:128], AF.Exp)
        qs = E.qs.next()
        P.tt(qs.v(), E.qT[:, hd, ch * 128:(ch + 1) * 128], eb.v(), ALU.mult, eng="pool")
        po = E.ps_o.next()
        P.mm(po[:, 0:257], st.v(), va, start=True, stop=False)
        P.mm(po[:, 0:257], qs.v(), Cb.v(), start=False, stop=True)
        P.act(c[:, 6:7], po[:, 256:257], AF.Abs)
        P.ts(c[:, 4:5], c[:, 6:7], 1.0, ALU.max)
        P.recip(c[:, 5:6], c[:, 4:5])
        dst = hsum[:, ch, hd * 256:(hd + 1) * 256]
        if (ch, hd) in written:
            P.stt(dst, po[:, 0:256], c[:, 5:6], dst, ALU.mult, ALU.add)
        else:
            P.act(dst, po[:, 0:256], AF.Copy, scale=c[:, 5:6])
            written.add((ch, hd))
    vw = E.vw.next()
    P.ts(vw.v(), va, c[:, 2:3], ALU.mult, eng="pool")
    pc = E.ps_c.next()
    P.mm(pc[:, 0:257], E.k[:, ch, hd * 128:(hd + 1) * 128], vw.v())
    P.stt(Cf.v(), Cf.v(), c[:, 3:4], pc[:, 0:257], ALU.mult, ALU.add)
    P.copy(Cb.v(), Cf.v(), eng="pool")
    if mode == "A":
        j = hd * 2 + dr
        P.tt(E.etot[:, j:j + 1], E.etot[:, j:j + 1], c[:, 3:4], ALU.mult, eng="pool")


O1B_IN = {k: O1_OUT[k] for k in ("mqT", "mkT", "mk", "mva", "gts")}
O1B_OUT = {"hcs": ([LC, 1024], F32), "CnC": ([8, 128, 257], F32), "CnL": ([8, 128, 257], F32), "eBT": ([128, 8], F32)}


def phase_o1b(P, io):
    E = mlstm_setup(P, io)
    hsum = P.sb([128, NT128, 1024], F32, "hsum")
    written = set()
    for hd in range(4):
        for dr in range(2):
            P.memset(E.Cf[hd][dr].v(), 0.0, eng="pool"); P.memset(E.Cb[hd][dr].v(), 0.0, eng="pool")
    P.memset(E.etot.v(), 1.0, eng="pool")
    for i in range(2):
        for hd in range(4):
            for dr in range(2):
                ch = 8 + (i if dr == 0 else 1 - i)
                mlstm_step(P, E, hd, dr, ch, "B", hsum, written)
    P.dma(io["hcs"].v().rearrange("(t p) n -> p t n", p=128), hsum[:, 8:10, :])
    for hd in range(4):
        for dr in range(2):
            P.dma(io["CnC"][hd * 2 + dr], E.Cf[hd][dr].v(), queue="act" if dr else "sp")
    for hd in range(4):
        for dr in range(2):
            P.memset(E.Cf[hd][dr].v(), 0.0, eng="pool"); P.memset(E.Cb[hd][dr].v(), 0.0, eng="pool")
    for i in range(8):
        for hd in range(4):
            for dr in range(2):
                ch = i if dr == 0 else 7 - i
                mlstm_step(P, E, hd, dr, ch, "A")
    for hd in range(4):
        for dr in range(2):
            P.dma(io["CnL"][hd * 2 + dr], E.Cf[hd][dr].v(), queue="act" if dr else "sp")
    P.dma(io["eBT"].v(), E.etot.v())


O2A_IN = {**O1B_IN, "og": O1_OUT["og"], "hcs": O1B_OUT["hcs"], "CnC": O1B_OUT["CnC"],
          "FC": ([8, 7, 128, 257], F32), "FE": ([128, 8 * 7], F32), "hnw": ([1, 1024], F32), "ident": ([128, 128], F32)}


def phase_o2a(P, io, need_ctx=True):
    E = mlstm_setup(P, io)
    hsum = P.sb([128, NT128, 1024], F32, "hsum")
    written = set()
    fe = P.sb([128, 8, 7], F32, "foldE")
    P.dma(fe.v().rearrange("p a b -> p (a b)"), io["FE"].v())
    fcb = rot_sb(P, 2, [128, 7, 257], F32, "foldC")
    for hd in range(4):
        for dr in range(2):
            j = hd * 2 + dr
            Cf = E.Cf[hd][dr]
            P.dma(Cf.v(), io["CnC"][j], queue="act")
            fc = fcb.next()
            P.dma(fc.v(), io["FC"][j].rearrange("s p n -> p s n"))
            for sl in range(7):
                P.stt(Cf.v(), Cf.v(), fe[:, j, sl:sl + 1], fc[:, sl, :], ALU.mult, ALU.add)
            P.copy(E.Cb[hd][dr].v(), Cf.v(), eng="pool")
    for i in range(8):
        for hd in range(4):
            for dr in range(2):
                ch = i if dr == 0 else 7 - i
                mlstm_step(P, E, hd, dr, ch, "B", hsum, written)
    nt = NT128 if need_ctx else TOK // 128
    if need_ctx:
        P.dma(hsum[:, 8:10, :], io["hcs"].v().rearrange("(t p) n -> p t n", p=128))
    ident = P.sb([128, 128], F32, "ident"); P.dma(ident.v(), io["ident"].v())
    identb = P.sb([128, 128], BF16, "identb"); P.copy(identb.v(), ident.v())
    hnw = P.sb([128, 1024], F32, "hnw_bc")
    P.dma(hnw.v(), io["hnw"][0:1, :].partition_broadcast(128))
    ogt = rot_sb(P, 2, [128, 1024], F32, "ogt")
    junk = P.sb([128, 256], BF16, "hjunk")
    st = rot_sb(P, 4, [128, 4], F32, "hstat")
    tmp = rot_sb(P, 2, [128, 256], F32, "htmp")
    ybf = rot_sb(P, 2, [128, 1024], BF16, "ybf")
    ps_t = rot_ps(P, 1, [128, 1024], BF16, "ps_ty")
    ystg = P.sb([128, 8, NLOC], BF16, "ystg")
    if not need_ctx:
        P.memset(ystg[:, :, TOK:], 0.0, eng="pool")
    for t in range(nt):
        og = ogt.next(); yb = ybf.next()
        P.dma(og.v(), io["og"][t * 128:(t + 1) * 128, :], queue="sp" if t % 2 == 0 else "act")
        for hd in range(4):
            ss = st.next(); tm = tmp.next()
            hv = hsum[:, t, hd * 256:(hd + 1) * 256]
            P.act(junk.v(), hv, AF.Square, accum_out=ss[:, 0:1])
            P.ts(ss[:, 1:2], ss[:, 0:1], 1.0 / 256, ALU.mult, EPS, ALU.add)
            P.act(ss[:, 2:3], ss[:, 1:2], AF.Sqrt)
            P.recip(ss[:, 3:4], ss[:, 2:3])
            P.stt(tm.v(), hv, ss[:, 3:4], hnw[:, hd * 256:(hd + 1) * 256], ALU.mult, ALU.mult)
            P.tt(yb[:, hd * 256:(hd + 1) * 256], tm.v(), og[:, hd * 256:(hd + 1) * 256], ALU.mult, eng="pool")
        pt = ps_t.next()
        for c in range(8):
            P.transpose(pt[:, c * 128:(c + 1) * 128], yb[:, c * 128:(c + 1) * 128], identb.v())
        P.copy(ystg[:, :, t * 128:(t + 1) * 128], pt.v().rearrange("p (c n) -> p c n", c=8), eng="act")
    for c in range(8):
        P.dma(io["mixT"][c], ystg[:, c, :], queue="sp" if c % 2 == 0 else "act")


O2B_IN = {"aqT": O1_OUT["aqT"], "azT": O1_OUT["azT"], "akx": ([2, 64, TOK + 256], BF16), "avx": ([TOK + 256, 128], BF16),
          "akc": ([2, 64, LC], BF16), "avc": ([LC, 128], BF16), "sink": ([1, 16], F32), "wm": ([4, 128, 128], F32)}


def phase_o2b(P, io, need_ctx=True):
    qd = ["sp", "act"]
    aq = P.sb([64, 16, NLOC], BF16, "aq"); az = P.sb([64, 16, NLOC], BF16, "az")
    for h in range(16):
        P.dma(aq[:, h, :], io["aqT"][h], queue=qd[h % 2]); P.dma(az[:, h, :], io["azT"][h], queue=qd[(h + 1) % 2])
    ak = P.sb([64, 2, TOK + 256], BF16, "akx"); akc = P.sb([64, 2, LC], BF16, "akc")
    P.dma(ak.v(), io["akx"].v().rearrange("g d t -> d g t")); P.dma(akc.v(), io["akc"].v().rearrange("g d t -> d g t"), queue="act")
    av = P.sb([128, 10, 128], BF16, "avx"); avc = P.sb([128, 2, 128], BF16, "avc")
    P.dma(av.v(), io["avx"].v().rearrange("(c p) n -> p c n", p=128)); P.dma(avc.v(), io["avc"].v().rearrange("(c p) n -> p c n", p=128), queue="act")
    ones = P.sb([128, 64], BF16, "ones64"); P.memset(ones.v(), 1.0)
    snk = P.sb([64, 16], F32, "sink_bc"); esnk = P.sb([64, 16], F32, "esink")
    P.dma(snk.v(), io["sink"][0:1, :].partition_broadcast(64))
    P.act(esnk.v(), snk.v(), AF.Exp)
    wmf = P.sb([128, 4, 128], F32, "wmf")
    P.dma(wmf.v(), io["wm"].v().rearrange("a p t -> p a t"))
    wm4 = P.sb([128, 4, 4, 128], BF16, "wm4")
    for kind in range(4):
        for rep in range(4):
            P.copy(wm4[:, kind, rep, :], wmf[:, kind, :], eng="pool")
    wst = P.sb([64, 16, NLOC], BF16, "wstage")
    ps_s = rot_ps(P, 3, [128, 512], F32, "wps_s")
    ps_o = rot_ps(P, 2, [64, 512], F32, "wps_o")
    ps_d = rot_ps(P, 2, [64, 512], F32, "wps_d")
    pT = rot_sb(P, 3, [128, 512], BF16, "wpT")
    dsb = rot_sb(P, 2, [64, 512], F32, "wden")
    osb = rot_sb(P, 2, [64, 512], F32, "wosb")
    scale = float(64 ** -0.5)
    nblk = 8 + (2 if need_ctx else 0)
    if not need_ctx:
        P.memset(wst[:, :, TOK:], 0.0, eng="pool")
    for n in range(nblk):
        for g in range(2):
            if n < 8:
                kinds = [(ak[:, g, n * 128:(n + 1) * 128], av[:, n, g * 64:(g + 1) * 64], 0 if n == 0 else 1),
                         (ak[:, g, (n + 1) * 128:(n + 2) * 128], av[:, n + 1, g * 64:(g + 1) * 64], None),
                         (ak[:, g, (n + 2) * 128:(n + 3) * 128], av[:, n + 2, g * 64:(g + 1) * 64], 3 if n == 7 else 2)]
            else:
                kinds = []
            for c in range(2):
                kinds.append((akc[:, g, c * 128:(c + 1) * 128], avc[:, c, g * 64:(g + 1) * 64], None))
            q0 = n * 128
            for half in range(2):
                h0 = g * 8 + half * 4
                po = ps_o.next(); pd = ps_d.next()
                for ci, (kv, vv, mk) in enumerate(kinds):
                    ps = ps_s.next()
                    P.mm(ps.v().rearrange("p (h q) -> p h q", h=4), kv, aq[:, h0:h0 + 4, q0:q0 + 128])
                    p = pT.next()
                    P.act(p.v(), ps.v(), AF.Exp, scale=scale)
                    if mk is not None:
                        P.tt(p.v(), p.v(), wm4[:, mk].rearrange("p h q -> p (h q)"), ALU.mult)
                    P.mm(po.v(), vv, p.v(), start=(ci == 0), stop=(ci == len(kinds) - 1))
                    P.mm(pd.v(), ones.v(), p.v(), start=(ci == 0), stop=(ci == len(kinds) - 1))
                ds = dsb.next(); os_ = osb.next()
                for hh in range(4):
                    P.ts(ds[:, hh * 128:(hh + 1) * 128], pd[:, hh * 128:(hh + 1) * 128], esnk[:, h0 + hh:h0 + hh + 1], ALU.add)
                P.recip(ds.v(), ds.v())
                P.tt(os_.v(), po.v(), ds.v(), ALU.mult)
                P.tt(wst[:, h0:h0 + 4, q0:q0 + 128], os_.v().rearrange("p (h q) -> p h q", h=4), az[:, h0:h0 + 4, q0:q0 + 128], ALU.mult, eng="pool")
    for h in range(16):
        P.dma(io["mixT"][8 + h // 2, (h % 2) * 64:(h % 2) * 64 + 64, :], wst[:, h, :], queue=qd[h % 2])


def window_masks(r):
    j = np.arange(128)[:, None]; i = np.arange(128)[None, :]
    prev = (j >= i).astype(np.float32); nxt = (j <= i).astype(np.float32)
    return np.stack([prev * (0.0 if r == 0 else 1.0), prev, nxt, nxt * (0.0 if r == NCORES - 1 else 1.0)])


def fold_lists(r, CnL_all, eBT_all):
    FC = np.zeros((8, 7, 128, 257), np.float32); FE = np.ones((128, 8, 7), np.float32)
    for hd in range(4):
        for dr in range(2):
            j = hd * 2 + dr
            segs = list(range(0, r)) if dr == 0 else list(range(NCORES - 1, r, -1))
            for sl, sg in enumerate(segs):
                FC[j, sl] = CnL_all[sg][j]; FE[:, j, sl] = eBT_all[sg][:, j]
    return FC, np.ascontiguousarray(FE.reshape(128, 56))


def o2b_halo(r, o1o):
    z = np.zeros((2, 64, 128), o1o[r]["akT"].dtype)
    zv = np.zeros((128, 128), o1o[r]["av"].dtype)
    kl = o1o[r - 1]["akT"][:, :, TOK - 128:TOK] if r > 0 else z
    kr = o1o[r + 1]["akT"][:, :, 0:128] if r < NCORES - 1 else z
    vl = o1o[r - 1]["av"][TOK - 128:TOK] if r > 0 else zv
    vr = o1o[r + 1]["av"][0:128] if r < NCORES - 1 else zv
    akx = np.concatenate([kl, o1o[r]["akT"][:, :, 0:TOK], kr], 2)
    avx = np.concatenate([vl, o1o[r]["av"][0:TOK], vr], 0)
    return (np.ascontiguousarray(akx), np.ascontiguousarray(avx), np.ascontiguousarray(o1o[r]["akT"][:, :, TOK:]),
            np.ascontiguousarray(o1o[r]["av"][TOK:]))


def _build(name, phases, ext_in, ext_out, specs):
    P = Prog(name); P.setup()
    io = mk_io(P, specs, set(ext_in), set(ext_out))
    for ph in phases:
        P.begin_phase()
        ph(P, io)
        P.end_phase()
    return P


def build_L_e1():
    return _build("e1", [phase_e1], E1_IN, E1_OUT, {**E1_IN, **E1_OUT})


def build_L_e2(final=False):
    spec_in = {**E2A_IN, **{k: v for k, v in OUT_IN.items() if k != "mixT"}}
    spec_out = dict(OUT_OUT)
    specs = {**spec_in, **E2A_OUT, **spec_out}
    return _build("e2", [phase_e2a, lambda P, io: phase_out(P, io, False)], spec_in, spec_out, specs)


def build_L_o1():
    ext_in = {**O1_IN, **MCONST_IN}
    ext_out = {**O1_OUT, **O1B_OUT}
    return _build("o1", [phase_o1, phase_o1b], ext_in, ext_out, {**ext_in, **ext_out})


def build_L_o2(need_ctx, final):
    spec_in = {**O2A_IN, **MCONST_IN, **O2B_IN, **{k: v for k, v in OUT_IN.items() if k != "mixT"}}
    spec_in = {k: v for k, v in spec_in.items()}
    if final:
        spec_in["fnw"] = ([1, D], F32)
        spec_out = {"out": ([TOK, D], F32)}
    else:
        spec_out = dict(OUT_OUT)
    specs = {**spec_in, "mixT": ([16, 128, NLOC], BF16), **spec_out}
    return _build("o2f" if final else "o2", [lambda P, io: phase_o2a(P, io, need_ctx), lambda P, io: phase_o2b(P, io, need_ctx),
                                             lambda P, io: phase_out(P, io, final)], spec_in, spec_out, specs)


def kernel(x, c, ctx, c_ctx, ada_w, ada_b, norm_w, ev_w_in, ev_q_norm, ev_kv_norm, ev_w_uq, ev_w_ukv,
           ev_w_pool, ev_pool_scale, ev_w_out, od_w_in, od_gate_b, od_head_norm, od_sink, od_w_out, final_norm):
    f32 = lambda a: np.ascontiguousarray(np.asarray(a, dtype=np.float32))
    x = f32(x); ctx = f32(ctx)
    mod = run_mods({"c": f32(c), "c_ctx": f32(c_ctx), "ada_w": f32(ada_w), "ada_b": f32(ada_b)})
    tabs = rope_tables()
    cosT, sinT, rmT = tabs
    ident = np.eye(128, dtype=np.float32)
    mconst = mlstm_consts()
    X = x[0]; XC = ctx[0]
    out = None
    for i in range(DEPTH):
        j = i // 2
        need_ctx = i < DEPTH - 1
        final = i == DEPTH - 1
        gate2 = np.ascontiguousarray(mod[i][:, 2 * D:3 * D])
        if i % 2 == 0:
            ins = [e1_inputs(r, X, XC, mod[i], f32(norm_w[i]), f32(ev_w_in[j]), f32(ev_q_norm[j]), f32(ev_kv_norm[j]),
                             f32(ev_w_uq[j]), f32(ev_w_ukv[j]), tabs) for r in range(NCORES)]
            e1o = run(build_L_e1(), ins)
            KN, KR, V = gather_kv(e1o)
            ins = []
            for r in range(NCORES):
                d = e2a_inputs(r, e1o, KN, KR, V, f32(ev_w_pool[j]), f32(ev_pool_scale[j]))
                d.update({"w_out": f32(ev_w_out[j]), "x": np.ascontiguousarray(np.concatenate([X[r * TOK:(r + 1) * TOK], XC], 0)),
                          "gate": gate2})
                ins.append(d)
            res = run(build_L_e2(), ins)
        else:
            ins = []
            for r in range(NCORES):
                d = {"x": np.ascontiguousarray(np.concatenate([X[r * TOK:(r + 1) * TOK], XC], 0)), "ident": ident,
                     "cosT": np.ascontiguousarray(cosT[:, r * TOK:(r + 1) * TOK]),
                     "sinT": np.ascontiguousarray(sinT[:, r * TOK:(r + 1) * TOK]), "rmT": rmT,
                     "modT": mod_cols(mod[i]), "normT": colT(f32(norm_w[i]), KC), "w_in": f32(od_w_in[j]),
                     "gate_b": f32(od_gate_b[j])[None, :]}
                d.update(mconst)
                ins.append(d)
            o1o = run(build_L_o1(), ins)
            CnL_all = [o["CnL"] for o in o1o]; eBT_all = [o["eBT"] for o in o1o]
            ins = []
            for r in range(NCORES):
                o = o1o[r]
                FC, FE = fold_lists(r, CnL_all, eBT_all)
                akx, avx, akc, avc = o2b_halo(r, o1o)
                d = {k: o[k] for k in O1B_IN}
                d.update({"og": o["og"], "hcs": o["hcs"], "CnC": o["CnC"], "FC": FC, "FE": FE,
                          "hnw": f32(od_head_norm[j])[None, :], "ident": ident,
                          "aqT": o["aqT"], "azT": o["azT"], "akx": akx, "avx": avx, "akc": akc, "avc": avc,
                          "sink": f32(od_sink[j])[None, :], "wm": window_masks(r),
                          "w_out": f32(od_w_out[j]), "x": np.ascontiguousarray(np.concatenate([X[r * TOK:(r + 1) * TOK], XC], 0)),
                          "gate": gate2})
                d.update(mconst)
                if final:
                    d["fnw"] = f32(final_norm)[None, :]
                ins.append(d)
            res = run(build_L_o2(need_ctx, final), ins)
        if final:
            out = np.concatenate([res[r]["out"] for r in range(NCORES)], 0)
        else:
            X = np.concatenate([res[r]["xo"][0:TOK] for r in range(NCORES)], 0)
            XC = np.ascontiguousarray(res[0]["xo"][TOK:])
    return np.ascontiguousarray(out[None].astype(np.float32))
```

```python
import contextlib
import numpy as np
import concourse.bass as bass
import concourse.mybir as mybir
from concourse.bass_utils import run_bass_kernel_spmd

F32 = mybir.dt.float32
BF16 = mybir.dt.bfloat16
ALU = mybir.AluOpType
AF = mybir.ActivationFunctionType
AX = mybir.AxisListType

ENGS = ("pe", "act", "dve", "pool", "sp")
N_DMA_SEMS = 24


class Tile:
    def __init__(self, handle, name, space):
        self.h = handle
        self.name = name
        self.space = space
        self.last_w = None
        self.readers = []

    def __getitem__(self, idx):
        return View(self, self.h[idx])

    def v(self):
        return View(self, self.h if self.space == "dram" else self.h[:])


class View:
    def __init__(self, tile, ap):
        self.tile = tile
        self.ap = ap

    def __getitem__(self, idx):
        return View(self.tile, self.ap[idx])

    def rearrange(self, *a, **k):
        return View(self.tile, self.ap.rearrange(*a, **k))

    def bitcast(self, dt):
        return View(self.tile, self.ap.bitcast(dt))

    def partition_broadcast(self, n):
        return View(self.tile, self.ap.partition_broadcast(n))


class Prog:
    def __init__(self, name="k"):
        self.nc = bass.Bass("TRN2", target_bir_lowering=False, name=name)
        self.stack = contextlib.ExitStack()
        self.ops = {e: [] for e in ENGS}
        self.cnt = {e: 0 for e in ENGS}
        self.esem = {}
        self.waited = {e: {} for e in ENGS}
        self.dma_sems = []
        self.dma_rr = 0
        self.sems = {}
        self.out_tokens = []
        self.n_tiles = 0
        self.pending = {e: [] for e in ENGS}
        self.phase_stack = None
        self.phase_id = 0

    def _sem(self, name):
        s = self.stack.enter_context(self.nc.semaphore(name))
        self.sems[name] = s
        return s

    def setup(self):
        for e in ("pe", "act", "dve", "pool"):
            self.esem[e] = self._sem("e_" + e)
        for i in range(N_DMA_SEMS):
            self.dma_sems.append([self._sem(f"d{i}"), 0, None])

    def begin_phase(self):
        self.phase_stack = contextlib.ExitStack()
        self.phase_id += 1

    def end_phase(self):
        self.barrier()
        self.phase_stack.close()
        self.phase_stack = None

    def barrier(self):
        toks = []
        for e in ("pe", "act", "dve", "pool"):
            if self.cnt[e]:
                toks.append(("e", e, self.cnt[e]))
        for ent in self.dma_sems:
            if ent[2] is not None:
                toks.append(ent[2])
        for e in ENGS:
            self.pending[e].extend(self._waits_for(e, toks))

    def _tstack(self):
        return self.phase_stack if self.phase_stack is not None else self.stack

    def sb(self, shape, dt=F32, name=None):
        self.n_tiles += 1
        name = (name or "t") + f"_{self.phase_id}_{self.n_tiles}"
        h = self._tstack().enter_context(self.nc.sbuf_tensor("s_" + name, list(shape), dt))
        return Tile(h, name, "sbuf")

    def ps(self, shape, dt=F32, name=None):
        self.n_tiles += 1
        name = (name or "p") + f"_{self.phase_id}_{self.n_tiles}"
        h = self._tstack().enter_context(self.nc.psum_tensor("q_" + name, list(shape), dt))
        return Tile(h, name, "psum")

    def dram(self, name, shape, dt=F32, kind="Internal"):
        h = self.nc.dram_tensor(name, list(shape), dt, kind=kind).ap()
        return Tile(h, name, "dram")

    def _deps(self, reads, writes):
        toks = []
        for v in reads:
            t = v.tile if isinstance(v, View) else v
            if t.last_w is not None:
                toks.append(t.last_w)
        for v in writes:
            t = v.tile if isinstance(v, View) else v
            if t.last_w is not None:
                toks.append(t.last_w)
            toks.extend(t.readers)
        return toks

    def _commit(self, tok, reads, writes):
        for v in reads:
            t = v.tile if isinstance(v, View) else v
            t.readers.append(tok)
        for v in writes:
            t = v.tile if isinstance(v, View) else v
            t.last_w = tok
            t.readers = []

    def _waits_for(self, eng, toks, same_engine_pe=False):
        need = {}
        for tok in toks:
            key = (tok[0], tok[1])
            if need.get(key, 0) < tok[2]:
                need[key] = tok[2]
        out = []
        for key, val in need.items():
            if key[0] == "e" and key[1] == eng and (eng == "pe" or same_engine_pe):
                continue
            if self.waited[eng].get(key, 0) >= val:
                continue
            self.waited[eng][key] = val
            sem = self.esem[key[1]] if key[0] == "e" else self.dma_sems[key[1]][0]
            out.append((sem, val))
        return out

    def op(self, eng, fn, reads=(), writes=(), extra=()):
        toks = self._deps(reads, writes) + list(extra)
        waits = self.pending[eng] + self._waits_for(eng, toks)
        self.pending[eng] = []
        self.cnt[eng] += 1
        tok = ("e", eng, self.cnt[eng])
        self.ops[eng].append((waits, fn, (self.esem[eng], 1)))
        self._commit(tok, reads, writes)
        return tok

    def dma(self, out, in_, queue="sp", is_output=False, **kw):
        reads, writes = [in_], [out]
        toks = self._deps(reads, writes)
        i = self.dma_rr
        self.dma_rr = (self.dma_rr + 1) % N_DMA_SEMS
        ent = self.dma_sems[i]
        if ent[2] is not None:
            toks.append(ent[2])
        waits = self.pending[queue] + self._waits_for(queue, toks)
        self.pending[queue] = []
        ent[1] += 16
        tok = ("d", i, ent[1])
        ent[2] = tok
        oap, iap = out.ap, in_.ap

        def fn(e, oap=oap, iap=iap, kw=kw):
            return e.dma_start(out=oap, in_=iap, **kw)

        self.ops[queue].append((waits, fn, (ent[0], 16)))
        self._commit(tok, reads, writes)
        if is_output:
            self.out_tokens.append(tok)
        return tok

    def mm(self, out, lhsT, rhs, start=True, stop=True, **kw):
        def fn(e):
            return e.matmul(out.ap, lhsT.ap, rhs.ap, start=start, stop=stop, **kw)
        return self.op("pe", fn, reads=[lhsT, rhs], writes=[out])

    def transpose(self, out, in_, ident):
        def fn(e):
            return e.transpose(out.ap, in_.ap, ident.ap)
        return self.op("pe", fn, reads=[in_, ident], writes=[out])

    def act(self, out, in_, func, bias=None, scale=1.0, accum_out=None, eng="act"):
        reads = [in_]
        kw = {}
        if isinstance(bias, View):
            reads.append(bias); kw["bias"] = bias.ap
        elif bias is not None:
            kw["bias"] = bias
        if isinstance(scale, View):
            reads.append(scale); kw["scale"] = scale.ap
        else:
            kw["scale"] = scale
        writes = [out]
        if accum_out is not None:
            writes.append(accum_out); kw["accum_out"] = accum_out.ap

        def fn(e):
            return e.activation(out.ap, in_.ap, func, **kw)
        return self.op("act", fn, reads=reads, writes=writes)

    def tt(self, out, in0, in1, op, eng="dve"):
        def fn(e):
            return e.tensor_tensor(out.ap, in0.ap, in1.ap, op)
        return self.op(eng, fn, reads=[in0, in1], writes=[out])

    def ts(self, out, in0, s1, op0, s2=None, op1=None, eng="dve", accum_out=None):
        reads = [in0]
        a1 = s1.ap if isinstance(s1, View) else s1
        a2 = s2.ap if isinstance(s2, View) else s2
        if isinstance(s1, View): reads.append(s1)
        if isinstance(s2, View): reads.append(s2)
        writes = [out]
        kw = {}
        if accum_out is not None:
            writes.append(accum_out); kw["accum_out"] = accum_out.ap

        def fn(e):
            if op1 is None:
                return e.tensor_scalar(out.ap, in0.ap, a1, None, op0, **kw)
            return e.tensor_scalar(out.ap, in0.ap, a1, a2, op0, op1, **kw)
        return self.op(eng, fn, reads=reads, writes=writes)

    def stt(self, out, in0, scalar, in1, op0, op1, eng="dve"):
        reads = [in0, in1]
        a = scalar.ap if isinstance(scalar, View) else scalar
        if isinstance(scalar, View): reads.append(scalar)

        def fn(e):
            return e.scalar_tensor_tensor(out.ap, in0.ap, a, in1.ap, op0, op1)
        return self.op("dve", fn, reads=reads, writes=[out])

    def copy(self, out, in_, eng="dve"):
        if eng == "act":
            def fn(e):
                return e.copy(out.ap, in_.ap)
        else:
            def fn(e):
                return e.tensor_copy(out.ap, in_.ap)
        return self.op(eng, fn, reads=[in_], writes=[out])

    def memset(self, out, val, eng="dve"):
        def fn(e):
            return e.memset(out.ap, val)
        return self.op(eng, fn, reads=[], writes=[out])

    def recip(self, out, in_):
        def fn(e):
            return e.reciprocal(out.ap, in_.ap)
        return self.op("dve", fn, reads=[in_], writes=[out])

    def reduce(self, out, in_, op, axis=AX.X, eng="dve"):
        def fn(e):
            return e.tensor_reduce(out.ap, in_.ap, axis, op)
        return self.op(eng, fn, reads=[in_], writes=[out])

    def finish(self):
        nc = self.nc
        fin = list(self.out_tokens)
        for e in ("pe", "act", "dve", "pool"):
            if self.cnt[e]:
                fin.append(("e", e, self.cnt[e]))
        for i, ent in enumerate(self.dma_sems):
            if ent[2] is not None:
                fin.append(ent[2])
        fwaits = self.pending["sp"] + self._waits_for("sp", fin)
        engmap = {"pe": "tensor", "act": "scalar", "dve": "vector", "pool": "gpsimd", "sp": "sync"}
        with nc.Block() as block:
            for e in ENGS:
                ops = self.ops[e]
                last = fwaits if e == "sp" else []
                if not ops and not last:
                    continue

                def body(eng, ops=ops, last=last):
                    for waits, fn, inc in ops:
                        for sem, val in waits:
                            eng.wait_ge(sem, val)
                        ins = fn(eng)
                        ins.then_inc(inc[0], inc[1])
                    for sem, val in last:
                        eng.wait_ge(sem, val)
                getattr(block, engmap[e])(body)
        self.stack.close()
        return nc


NCORES = 8
D = 2048
SEQ = 8192
LC = 256
TOK = SEQ // NCORES
DEPTH = 4
EPS = 1e-6
KC = D // 128


def run(prog, in_maps):
    nc = prog.finish()
    res = run_bass_kernel_spmd(nc, in_maps, core_ids=list(range(NCORES)))
    return res.results


def build_mods():
    P = Prog("mods"); P.setup()
    NCOL = 3 * D // NCORES
    cc = P.dram("cc", [128, KC * 2], F32, kind="ExternalInput")
    w = P.dram("w", [DEPTH, D, NCOL], F32, kind="ExternalInput")
    b = P.dram("b", [DEPTH, 2, NCOL], F32, kind="ExternalInput")
    o = P.dram("mod", [DEPTH, 2, NCOL], F32, kind="ExternalOutput")
    cc_sb = P.sb([128, KC * 2], F32, "cc_sb")
    sc = P.sb([128, KC * 2], F32, "sc")
    P.dma(cc_sb.v(), cc.v())
    P.act(sc.v(), cc_sb.v(), AF.Silu)
    wt = [P.sb([128, KC, NCOL], F32, f"wt{i}") for i in range(2)]
    bt = [P.sb([2, NCOL], F32, f"bt{i}") for i in range(2)]
    ot = [P.sb([2, NCOL], F32, f"ot{i}") for i in range(2)]
    pp = [P.ps([2, 512], F32, f"pp{i}") for i in range(4)]
    for i in range(DEPTH):
        wb = wt[i % 2]
        for j0 in range(0, KC, 4):
            P.dma(wb[:, j0:j0 + 4, :], w[i, j0 * 128:(j0 + 4) * 128, :].rearrange("(j p) n -> p j n", p=128),
                  queue="sp")
        P.dma(bt[i % 2].v(), b[i])
        for ci, (c0, c1) in enumerate(((0, 512), (512, NCOL))):
            ps = pp[(2 * i + ci) % 4]
            for j in range(KC):
                P.mm(ps[:, 0:c1 - c0], sc[:, 2 * j:2 * j + 2], wb[:, j, c0:c1], start=(j == 0), stop=(j == KC - 1))
            P.tt(ot[i % 2][:, c0:c1], ps[:, 0:c1 - c0], bt[i % 2][:, c0:c1], ALU.add)
        P.dma(o[i], ot[i % 2].v(), is_output=True)
    return P


def run_mods(inp):
    c, c_ctx, ada_w, ada_b = inp["c"], inp["c_ctx"], inp["ada_w"], inp["ada_b"]
    NCOL = 3 * D // NCORES
    cc = np.stack([c.reshape(D), c_ctx.reshape(D)], 0)
    cc_l = np.ascontiguousarray(cc.reshape(2, KC, 128).transpose(2, 1, 0).reshape(128, KC * 2))
    in_maps = []
    for r in range(NCORES):
        sl = slice(r * NCOL, (r + 1) * NCOL)
        in_maps.append({"cc": cc_l,
                        "w": np.ascontiguousarray(ada_w[:, :, sl]),
                        "b": np.ascontiguousarray(np.broadcast_to(ada_b[:, None, sl], (DEPTH, 2, NCOL)))})
    res = run(build_mods(), in_maps)
    mod = np.concatenate([res[r]["mod"] for r in range(NCORES)], axis=2)
    return mod


class Rot:
    def __init__(self, tiles):
        self.tiles = tiles
        self.i = 0

    def next(self):
        t = self.tiles[self.i % len(self.tiles)]
        self.i += 1
        return t


def rot_sb(P, n, shape, dt, name):
    return Rot([P.sb(shape, dt, f"{name}{i}") for i in range(n)])


def rot_ps(P, n, shape, dt, name):
    return Rot([P.ps(shape, dt, f"{name}{i}") for i in range(n)])


NLOC = TOK + LC
TT = [(0, 512), (512, 512), (1024, 256)]

POOL_W = 512; Q_RANK = 512; KV_RANK = 512; ROPE = 64; MLA_H = 12; MLA_NOPE = 128; MLA_V = 128
EV_IN = 3648
NKEY = SEQ + LC
NKC = NKEY // 128


class WStream:
    def __init__(self, P, stage_elems=4096, nstage=2, nbf=2):
        self.P = P
        self.n = stage_elems
        self.stage = rot_sb(P, nstage, [128, stage_elems], F32, "wstg")
        self.bf = rot_sb(P, nbf, [128, stage_elems], BF16, "wbf")
        self.k = 0

    def load(self, src_view, kch, ncols, rows=128):
        P = self.P
        assert kch * ncols <= self.n
        st = self.stage.next()
        bf = self.bf.next()
        sv = st[0:rows, 0:kch * ncols].rearrange("p (j n) -> p j n", j=kch)
        bv = bf[0:rows, 0:kch * ncols].rearrange("p (j n) -> p j n", j=kch)
        half = max(1, kch // 2)
        P.dma(sv[:, 0:half, :], src_view[:, 0:half, :], queue="sp")
        if half < kch:
            P.dma(sv[:, half:kch, :], src_view[:, half:kch, :], queue="act")
        eng = "dve" if self.k % 2 == 0 else "pool"
        self.k += 1
        P.copy(bf[0:rows, 0:kch * ncols], st[0:rows, 0:kch * ncols], eng=eng)
        return bv


def emit_hT(P, x_dram, modT, hT, ident, ps_tr):
    gcol, scol = modT
    xt = rot_sb(P, 2, [128, D], F32, "xt")
    xn = rot_sb(P, 2, [128, D], F32, "xn")
    junk = P.sb([128, D], BF16, "sqjunk")
    st = rot_sb(P, 4, [128, 4], F32, "nstat")
    for t in range(NLOC // 128):
        s = 0 if t < TOK // 128 else 1
        x = xt.next(); n_ = xn.next(); ss = st.next()
        P.dma(x.v(), x_dram[t * 128:(t + 1) * 128, :], queue="sp" if t % 2 == 0 else "act")
        P.act(junk.v(), x.v(), AF.Square, accum_out=ss[:, 0:1])
        P.ts(ss[:, 1:2], ss[:, 0:1], 1.0 / D, ALU.mult, EPS, ALU.add)
        P.act(ss[:, 2:3], ss[:, 1:2], AF.Sqrt)
        P.recip(ss[:, 3:4], ss[:, 2:3])
        P.act(n_.v(), x.v(), AF.Copy, scale=ss[:, 3:4])
        for j4 in range(KC // 4):
            pt = ps_tr.next()
            for jj in range(4):
                j = j4 * 4 + jj
                P.transpose(pt[:, jj * 128:(jj + 1) * 128], n_[:, j * 128:(j + 1) * 128], ident.v())
            for jj in range(4):
                j = j4 * 4 + jj
                if jj % 2 == 0:
                    P.ts(hT[:, j, t * 128:(t + 1) * 128], pt[:, jj * 128:(jj + 1) * 128],
                         gcol[:, s, j:j + 1], ALU.mult, scol[:, s, j:j + 1], ALU.add)
                else:
                    P.act(hT[:, j, t * 128:(t + 1) * 128], pt[:, jj * 128:(jj + 1) * 128], AF.Identity,
                          bias=scol[:, s, j:j + 1], scale=gcol[:, s, j:j + 1])


def load_mod_cols(P, modT_dram, normT_dram):
    m = P.sb([128, 2, 3, KC], F32, "modT")
    nw = P.sb([128, KC], F32, "normT")
    g = P.sb([128, 2, KC], F32, "gcol")
    P.dma(m.v().rearrange("p a b c -> p (a b c)"), modT_dram.v())
    P.dma(nw.v(), normT_dram.v())
    for s in range(2):
        P.stt(g[:, s, :], m[:, s, 1, :], 1.0, nw.v(), ALU.add, ALU.mult)
    return g, m[:, :, 0, :], m[:, :, 2, :]


def emit_rope(P, out_bf, src_f32, n, tok0, cosT, sinT, rmT, ps_rot, tmp):
    pr = ps_rot.next()
    P.mm(pr[0:64, 0:n], rmT.v(), src_f32)
    t1 = tmp.next(); t2 = tmp.next()
    P.tt(t1[0:64, 0:n], src_f32, cosT[:, tok0:tok0 + n], ALU.mult, eng="pool")
    P.tt(t2[0:64, 0:n], pr[0:64, 0:n], sinT[:, tok0:tok0 + n], ALU.mult)
    P.tt(out_bf, t1[0:64, 0:n], t2[0:64, 0:n], ALU.add)


def phase_e1(P, io):
    x_in = io["x"]
    ident_d = io["ident"]
    ident = P.sb([128, 128], F32, "ident")
    P.dma(ident.v(), ident_d.v())
    ones_bf = P.sb([128, 128], BF16, "ones_bf")
    P.memset(ones_bf.v(), 1.0)
    epsc = P.sb([128, 1], F32, "epsc")
    P.memset(epsc.v(), EPS)
    cosT = P.sb([64, TOK], F32, "cosT"); sinT = P.sb([64, TOK], F32, "sinT"); rmT = P.sb([64, 64], F32, "rmT")
    P.dma(cosT.v(), io["cosT"].v()); P.dma(sinT.v(), io["sinT"].v(), queue="act"); P.dma(rmT.v(), io["rmT"].v())
    qn_c = P.sb([128, 4], F32, "qn_c"); kvn_c = P.sb([128, 4], F32, "kvn_c")
    P.dma(qn_c.v(), io["qnormT"].v()); P.dma(kvn_c.v(), io["kvnormT"].v())
    gcol, scol, _ = load_mod_cols(P, io["modT"], io["normT"])
    hT = P.sb([128, KC, NLOC], BF16, "hT")
    ps_tr = rot_ps(P, 2, [128, 512], F32, "ps_tr")
    emit_hT(P, x_in, (gcol, scol), hT, ident, ps_tr)

    ws = WStream(P)
    ps_mm = rot_ps(P, 3, [128, 512], F32, "ps_mm")
    ps_rot = rot_ps(P, 1, [64, 512], F32, "ps_rot")
    ps_n = rot_ps(P, 1, [128, 512], F32, "ps_n")
    ostg_f = rot_sb(P, 2, [128, NLOC], F32, "ostg_f")
    ostg_b = rot_sb(P, 3, [128, NLOC], BF16, "ostg_b")
    rtmp = rot_sb(P, 4, [64, 512], F32, "rtmp")
    c_sb = P.sb([128, 4, NLOC], F32, "c_sb")
    cn_bf = P.sb([128, 4, NLOC], BF16, "cn_bf")
    sq_bf = P.sb([128, 4, 512], BF16, "sq_bf")
    rstd = P.sb([128, 512], F32, "rstd")
    w_in = io["w_in"]

    def w_in_view(c0, nc_):
        return w_in[:, c0:c0 + nc_].rearrange("(j p) n -> p j n", p=128)

    def proj_chunk(wbf, cl, m, evac):
        for ti, (t0, n) in enumerate(TT):
            ps = ps_mm.next()
            for j in range(KC):
                P.mm(ps[0:m, 0:n], wbf[:, j, cl:cl + m], hT[:, j, t0:t0 + n], start=(j == 0), stop=(j == KC - 1))
            evac(ps[0:m, 0:n], ti, t0, n)

    qdma = ["sp", "act"]
    for gi in range(4):
        wbf = ws.load(w_in_view(gi * 256, 256), KC, 256)
        for c in range(2):
            ch = gi * 2 + c
            if ch < 4:
                stg = ostg_f.next()
                proj_chunk(wbf, c * 128, 128, lambda ps, ti, t0, n, stg=stg: P.copy(stg[:, t0:t0 + n], ps, eng="dve"))
                P.dma(io["pin"][ch], stg.v(), queue=qdma[ch % 2])
            else:
                stg = ostg_f.next()
                proj_chunk(wbf, c * 128, 128, lambda ps, ti, t0, n, stg=stg: P.act(stg[:, t0:t0 + n], ps, AF.Silu))
                P.dma(io["pgs"][ch - 4], stg.v(), queue=qdma[ch % 2])
    for gi in range(6):
        wbf = ws.load(w_in_view(2112 + gi * 256, 256), KC, 256)
        for c in range(2):
            ch = gi * 2 + c
            stg = ostg_b.next()
            proj_chunk(wbf, c * 128, 128, lambda ps, ti, t0, n, stg=stg: P.act(stg[:, t0:t0 + n], ps, AF.Silu))
            P.dma(io["ags"][ch], stg.v(), queue=qdma[ch % 2])
    wbf = ws.load(w_in_view(2048, 64), KC, 64)
    stg = ostg_b.next()
    krf = P.sb([64, NLOC], F32, "krf")

    def ev_kr(ps, ti, t0, n):
        P.copy(krf[:, t0:t0 + n], ps, eng="dve")
        if t0 < TOK:
            emit_rope(P, stg[0:64, t0:t0 + n], krf[:, t0:t0 + n], n, t0, cosT, sinT, rmT, ps_rot, rtmp)
        else:
            P.copy(stg[0:64, t0:t0 + n], krf[:, t0:t0 + n], eng="pool")
    proj_chunk(wbf, 0, 64, ev_kr)
    P.dma(io["kr"].v(), stg[0:64, :])

    for which in range(2):
        cbase = 1024 + which * 512
        for gi in range(2):
            wbf = ws.load(w_in_view(cbase + gi * 256, 256), KC, 256)
            for c in range(2):
                ch = gi * 2 + c
                proj_chunk(wbf, c * 128, 128, lambda ps, ti, t0, n, ch=ch: P.copy(c_sb[:, ch, t0:t0 + n], ps, eng="dve"))
        ncol = qn_c if which == 0 else kvn_c
        for ti, (t0, n) in enumerate(TT):
            for j in range(4):
                P.act(sq_bf[:, j, 0:n], c_sb[:, j, t0:t0 + n], AF.Square)
            pn = ps_n.next()
            for j in range(4):
                P.mm(pn[:, 0:n], ones_bf.v(), sq_bf[:, j, 0:n], start=(j == 0), stop=(j == 3))
            P.act(rstd[:, 0:n], pn[:, 0:n], AF.Sqrt, bias=epsc[:, 0:1], scale=1.0 / 512)
            P.recip(rstd[:, 0:n], rstd[:, 0:n])
            for j in range(4):
                P.stt(cn_bf[:, j, t0:t0 + n], c_sb[:, j, t0:t0 + n], ncol[:, j:j + 1], rstd[:, 0:n], ALU.mult, ALU.mult,
                      eng="dve" if j % 2 == 0 else "pool")
        if which == 0:
            w_uq = io["w_uq"]
            for hg in range(3):
                wbf = ws.load(w_uq[:, hg * 768:(hg + 1) * 768].rearrange("(j p) n -> p j n", p=128), 4, 768)
                for hh in range(4):
                    h = hg * 4 + hh
                    stg = ostg_b.next()
                    for ti, (t0, n) in enumerate(TT):
                        ps = ps_mm.next()
                        for j in range(4):
                            P.mm(ps[:, 0:n], wbf[:, j, hh * 192:hh * 192 + 128], cn_bf[:, j, t0:t0 + n], start=(j == 0), stop=(j == 3))
                        P.copy(stg[:, t0:t0 + n], ps[:, 0:n], eng="act")
                    P.dma(io["qn"][h], stg.v(), queue=qdma[h % 2])
                    stg = ostg_b.next()
                    for ti, (t0, n) in enumerate(TT):
                        ps = ps_mm.next()
                        for j in range(4):
                            P.mm(ps[0:64, 0:n], wbf[:, j, hh * 192 + 128:hh * 192 + 192], cn_bf[:, j, t0:t0 + n], start=(j == 0), stop=(j == 3))
                        if t0 < TOK:
                            rf = rtmp.next()
                            P.copy(rf[0:64, 0:n], ps[0:64, 0:n], eng="act")
                            emit_rope(P, stg[0:64, t0:t0 + n], rf[0:64, 0:n], n, t0, cosT, sinT, rmT, ps_rot, rtmp)
                        else:
                            P.copy(stg[0:64, t0:t0 + n], ps[0:64, 0:n], eng="act")
                    P.dma(io["qr"][h], stg[0:64, :], queue=qdma[(h + 1) % 2])
        else:
            w_ukv = io["w_ukv"]
            wv5 = w_ukv.v().rearrange("(j p) (h two c) -> p j h two c", p=128, two=2, c=128)
            for hg in range(2):
                wbf = None
                st_ = ws.stage.next(); bf_ = ws.bf.next()
                sv = st_[:, 0:4 * 768].rearrange("p (j h c) -> p j h c", j=4, h=6)
                for j in range(4):
                    P.dma(sv[:, j], wv5[:, j, hg * 6:(hg + 1) * 6, 0, :], queue=qdma[j % 2])
                P.copy(bf_[:, 0:4 * 768], st_[:, 0:4 * 768], eng="dve")
                wbf = bf_[:, 0:4 * 768].rearrange("p (j n) -> p j n", j=4)
                for hh in range(6):
                    h = hg * 6 + hh
                    stg = ostg_b.next()
                    for ti, (t0, n) in enumerate(TT):
                        ps = ps_mm.next()
                        for j in range(4):
                            P.mm(ps[:, 0:n], wbf[:, j, hh * 128:(hh + 1) * 128], cn_bf[:, j, t0:t0 + n], start=(j == 0), stop=(j == 3))
                        P.copy(stg[:, t0:t0 + n], ps[:, 0:n], eng="act")
                    P.dma(io["kn"][h], stg.v(), queue=qdma[h % 2])
            vst = rot_sb(P, 2, [128, 1536], BF16, "vst")
            wvb = []
            for hg in range(2):
                st_ = ws.stage.next(); bf_ = ws.bf.next()
                sv = st_[:, 0:4 * 768].rearrange("p (j h c) -> p j h c", j=4, h=6)
                for j in range(4):
                    P.dma(sv[:, j], wv5[:, j, hg * 6:(hg + 1) * 6, 1, :], queue=qdma[j % 2])
                P.copy(bf_[:, 0:4 * 768], st_[:, 0:4 * 768], eng="pool")
                wvb.append(bf_[:, 0:4 * 768].rearrange("p (j n) -> p j n", j=4))
            for t in range(NLOC // 128):
                vs = vst.next()
                for hg in range(2):
                    for half in range(2):
                        ps = ps_mm.next()
                        for j in range(4):
                            P.mm(ps[:, 0:384], cn_bf[:, j, t * 128:(t + 1) * 128], wvb[hg][:, j, half * 384:(half + 1) * 384],
                                 start=(j == 0), stop=(j == 3))
                        P.copy(vs[:, hg * 768 + half * 384:hg * 768 + (half + 1) * 384], ps[:, 0:384], eng="act" if half else "dve")
                P.dma(io["v"][t * 128:(t + 1) * 128, :], vs.v(), queue=qdma[t % 2])


def rope_tables():
    t = np.arange(SEQ)
    row = (t // 64).astype(np.float32); col = (t % 64).astype(np.float32)
    inv = (np.float32(10000.0) ** (-np.arange(0, 32, 2, dtype=np.float32) / np.float32(32))).astype(np.float32)
    ar = row[None, :] * inv[:, None]; ac = col[None, :] * inv[:, None]
    ang = np.concatenate([ar, ar, ac, ac], 0)
    cosT = np.cos(ang).astype(np.float32); sinT = np.sin(ang).astype(np.float32)
    rm = np.zeros((64, 64), np.float32)
    for b in (0, 32):
        for i in range(16):
            rm[b + i, b + i + 16] = -1.0
            rm[b + i + 16, b + i] = 1.0
    return cosT, sinT, np.ascontiguousarray(rm.T)


def colT(v, kc):
    return np.ascontiguousarray(np.asarray(v, np.float32).reshape(kc, 128).T)


def mod_cols(mod_i):
    m = mod_i.reshape(2, 3, KC, 128).transpose(3, 0, 1, 2).reshape(128, 2 * 3 * KC)
    return np.ascontiguousarray(m)


def mk_io(P, spec, ext_in, ext_out):
    io = {}
    for name, (shape, dt) in spec.items():
        kind = "ExternalInput" if name in ext_in else ("ExternalOutput" if name in ext_out else "Internal")
        io[name] = P.dram(name, shape, dt, kind=kind)
    return io


E1_IN = {"x": ([NLOC, D], F32), "ident": ([128, 128], F32), "cosT": ([64, TOK], F32), "sinT": ([64, TOK], F32),
         "rmT": ([64, 64], F32), "qnormT": ([128, 4], F32), "kvnormT": ([128, 4], F32),
         "modT": ([128, 2 * 3 * KC], F32), "normT": ([128, KC], F32),
         "w_in": ([D, EV_IN], F32), "w_uq": ([512, 2304], F32), "w_ukv": ([512, 3072], F32)}
E1_OUT = {"pin": ([4, 128, NLOC], F32), "pgs": ([4, 128, NLOC], F32), "ags": ([12, 128, NLOC], BF16),
          "kr": ([64, NLOC], BF16), "qn": ([12, 128, NLOC], BF16), "qr": ([12, 64, NLOC], BF16),
          "kn": ([12, 128, NLOC], BF16), "v": ([NLOC, 1536], BF16)}


def build_e1():
    P = Prog("e1"); P.setup()
    io = mk_io(P, {**E1_IN, **E1_OUT}, set(E1_IN), set(E1_OUT))
    phase_e1(P, io)
    return P


def e1_inputs(r, x, xc, mod_i, norm_w_i, w_in, q_norm, kv_norm, w_uq, w_ukv, tabs):
    cosT, sinT, rmT = tabs
    xl = np.concatenate([x[r * TOK:(r + 1) * TOK], xc], 0)
    return {"x": np.ascontiguousarray(xl), "ident": np.eye(128, dtype=np.float32),
            "cosT": np.ascontiguousarray(cosT[:, r * TOK:(r + 1) * TOK]),
            "sinT": np.ascontiguousarray(sinT[:, r * TOK:(r + 1) * TOK]), "rmT": rmT,
            "qnormT": colT(q_norm, 4), "kvnormT": colT(kv_norm, 4), "modT": mod_cols(mod_i), "normT": colT(norm_w_i, KC),
            "w_in": w_in, "w_uq": w_uq, "w_ukv": w_ukv}


E2A_IN = {"qn": ([12, 128, NLOC], BF16), "qr": ([12, 64, NLOC], BF16), "ags": ([12, 128, NLOC], BF16),
          "pgs": ([4, 128, NLOC], F32), "pinx": ([4, 128, TOK + 16], F32), "pinc": ([4, 128, LC + 16], F32),
          "rcl": ([128, 4, TOK], F32), "rcc": ([128, 4, LC], F32),
          "KN": ([12, 128, NKEY], BF16), "KR": ([64, NKEY], BF16), "V": ([NKEY, 1536], BF16),
          "w_pool": ([4, 128, 128], F32), "pscaleT": ([128, 4], F32)}
E2A_OUT = {"mixT": ([16, 128, NLOC], BF16)}


def phase_e2a(P, io):
    ones_bf = P.sb([128, 128], BF16, "ones_bf")
    P.memset(ones_bf.v(), 1.0)
    ps_s = rot_ps(P, 3, [128, 512], F32, "ps_s")
    ps_o = rot_ps(P, 2, [128, 512], F32, "ps_o")
    ps_d = rot_ps(P, 2, [128, 512], F32, "ps_d")
    ps_p = rot_ps(P, 1, [128, 512], F32, "ps_p")
    mstg = rot_sb(P, 3, [128, NLOC], BF16, "mstg")
    qd = ["sp", "act"]
    wp_f = P.sb([128, 4, 128], F32, "wp_f"); wp_b = P.sb([128, 4, 128], BF16, "wp_b")
    P.dma(wp_f.v(), io["w_pool"].v().rearrange("g c d -> c g d"))
    P.copy(wp_b.v(), wp_f.v(), eng="pool")
    psc = P.sb([128, 4], F32, "psc"); P.dma(psc.v(), io["pscaleT"].v())
    rcl = P.sb([128, 4, TOK], F32, "rcl"); rcc = P.sb([128, 4, LC], F32, "rcc")
    P.dma(rcl.v(), io["rcl"].v()); P.dma(rcc.v(), io["rcc"].v(), queue="act")
    U = rot_sb(P, 2, [128, TOK + 16], F32, "poolU")
    A = rot_sb(P, 2, [128, TOK + 16], F32, "poolA")
    zb = P.sb([128, NLOC], BF16, "poolz")
    pg = rot_sb(P, 2, [128, NLOC], F32, "poolpg")
    for g in range(4):
        w = 2 << g
        for (src, n, rc, t0) in ((io["pinx"], TOK, rcl, 0), (io["pinc"], LC, rcc, TOK)):
            u = U.next()
            P.dma(u[:, 0:n + 16], src[g], queue=qd[g % 2])
            cur = u; L = n + 16; step = 1
            while step < w:
                nxt = A.next()
                L2 = L - step
                P.tt(nxt[:, 0:L2], cur[:, 0:L2], cur[:, step:step + L2], ALU.add, eng="pool")
                cur = nxt; L = L2; step *= 2
            off = 8 - w // 2
            t1 = A.next()
            P.tt(t1[:, 0:n], cur[:, off:off + n], rc[:, g, :], ALU.mult)
            P.tt(zb[:, t0:t0 + n], t1[:, 0:n], u[:, 8:8 + n], ALU.subtract)
        pgt = pg.next()
        P.dma(pgt.v(), io["pgs"][g], queue=qd[(g + 1) % 2])
        stg = mstg.next()
        for (t0, n) in TT:
            ps = ps_p.next()
            P.mm(ps[:, 0:n], wp_b[:, g, :], zb[:, t0:t0 + n])
            P.stt(stg[:, t0:t0 + n], ps[:, 0:n], psc[:, g:g + 1], pgt[:, t0:t0 + n], ALU.mult, ALU.mult)
        P.dma(io["mixT"][g], stg.v(), queue=qd[g % 2])
    kr = P.sb([64, NKEY], BF16, "kr_all")
    P.dma(kr[:, 0:NKEY // 2], io["KR"][:, 0:NKEY // 2]); P.dma(kr[:, NKEY // 2:], io["KR"][:, NKEY // 2:], queue="act")
    knb = rot_sb(P, 2, [128, NKEY], BF16, "knb")
    vb = rot_sb(P, 2, [128, NKC, 128], BF16, "vb")
    qnb = rot_sb(P, 2, [128, NLOC], BF16, "qnb")
    qrb = rot_sb(P, 2, [64, NLOC], BF16, "qrb")
    agb = rot_sb(P, 2, [128, NLOC], BF16, "agb")
    pT = rot_sb(P, 3, [128, 512], BF16, "pT")
    rden = rot_sb(P, 2, [128, 512], F32, "rden")
    otmp = rot_sb(P, 2, [128, 512], F32, "otmp")
    scale = float(192 ** -0.5)
    for h in range(MLA_H):
        kn = knb.next(); v = vb.next(); qn = qnb.next(); qr = qrb.next(); ag = agb.next()
        hk = NKEY // 2
        P.dma(kn[:, 0:hk], io["KN"][h, :, 0:hk], queue="sp"); P.dma(kn[:, hk:], io["KN"][h, :, hk:], queue="act")
        vsrc = io["V"][:, h * 128:(h + 1) * 128].rearrange("(c p) d -> p c d", p=128)
        P.dma(v[:, 0:NKC // 2, :], vsrc[:, 0:NKC // 2, :], queue="sp"); P.dma(v[:, NKC // 2:, :], vsrc[:, NKC // 2:, :], queue="act")
        P.dma(qn.v(), io["qn"][h], queue="sp"); P.dma(qr.v(), io["qr"][h], queue="act"); P.dma(ag.v(), io["ags"][h], queue="sp")
        stg = mstg.next()
        for (t0, n) in TT:
            chunks = range(NKC) if t0 < TOK else range(SEQ // 128, NKC)
            po = ps_o.next(); pd = ps_d.next()
            nch = len(chunks)
            chl = list(chunks)

            def scores(c):
                ps = ps_s.next()
                P.mm(ps[:, 0:n], kn[:, c * 128:(c + 1) * 128], qn[:, t0:t0 + n], start=True, stop=False)
                P.mm(ps[:, 0:n], kr[:, c * 128:(c + 1) * 128], qr[:, t0:t0 + n], start=False, stop=True)
                return ps
            ps_next = scores(chl[0])
            for ci, c in enumerate(chl):
                ps = ps_next
                if ci + 1 < nch:
                    ps_next = scores(chl[ci + 1])
                p = pT.next()
                P.act(p[:, 0:n], ps[:, 0:n], AF.Exp, scale=scale)
                P.mm(po[:, 0:n], v[:, c, :], p[:, 0:n], start=(ci == 0), stop=(ci == nch - 1))
                P.mm(pd[:, 0:n], ones_bf.v(), p[:, 0:n], start=(ci == 0), stop=(ci == nch - 1))
            rd = rden.next(); ot = otmp.next()
            P.recip(rd[:, 0:n], pd[:, 0:n])
            P.tt(ot[:, 0:n], po[:, 0:n], rd[:, 0:n], ALU.mult)
            P.tt(stg[:, t0:t0 + n], ot[:, 0:n], ag[:, t0:t0 + n], ALU.mult, eng="pool")
        P.dma(io["mixT"][4 + h], stg.v(), queue=qd[h % 2])


def build_e2a():
    P = Prog("e2a"); P.setup()
    io = mk_io(P, {**E2A_IN, **E2A_OUT}, set(E2A_IN), set(E2A_OUT))
    phase_e2a(P, io)
    return P


POOL_WINDOWS = (2, 4, 8, 16)


def pool_rc(L):
    t = np.arange(L)
    out = np.zeros((4, L), np.float32)
    for g, w in enumerate(POOL_WINDOWS):
        lo = np.clip(t - w // 2, 0, L - 1); hi = np.clip(t + w // 2 - 1, 0, L - 1)
        out[g] = (1.0 / (hi - lo + 1).astype(np.float32)).astype(np.float32)
    return out


def e2a_inputs(r, e1o, KN, KR, V, w_pool, pool_scale):
    o = e1o[r]
    z8 = np.zeros((4, 128, 8), np.float32)
    left = e1o[r - 1]["pin"][:, :, TOK - 8:TOK] if r > 0 else z8
    right = e1o[r + 1]["pin"][:, :, 0:8] if r < NCORES - 1 else z8
    pinx = np.concatenate([left, o["pin"][:, :, 0:TOK], right], 2)
    pinc = np.concatenate([z8, o["pin"][:, :, TOK:], z8], 2)
    rcl = np.broadcast_to(pool_rc(SEQ)[None, :, r * TOK:(r + 1) * TOK], (128, 4, TOK))
    rcc = np.broadcast_to(pool_rc(LC)[None], (128, 4, LC))
    return {"qn": o["qn"], "qr": o["qr"], "ags": o["ags"], "pgs": o["pgs"], "pinx": np.ascontiguousarray(pinx),
            "pinc": np.ascontiguousarray(pinc), "rcl": np.ascontiguousarray(rcl), "rcc": np.ascontiguousarray(rcc),
            "KN": KN, "KR": KR, "V": V, "w_pool": w_pool, "pscaleT": colT(pool_scale, 4)}


def gather_kv(e1o):
    KN = np.concatenate([o["kn"][:, :, 0:TOK] for o in e1o] + [e1o[0]["kn"][:, :, TOK:]], 2)
    KR = np.concatenate([o["kr"][:, 0:TOK] for o in e1o] + [e1o[0]["kr"][:, TOK:]], 1)
    V = np.concatenate([o["v"][0:TOK] for o in e1o] + [e1o[0]["v"][TOK:]], 0)
    return np.ascontiguousarray(KN), np.ascontiguousarray(KR), np.ascontiguousarray(V)


OUT_IN = {"mixT": ([16, 128, NLOC], BF16), "w_out": ([D, D], F32), "x": ([NLOC, D], F32), "gate": ([2, D], F32)}
OUT_OUT = {"xo": ([NLOC, D], F32)}


def phase_out(P, io, final=False):
    qd = ["sp", "act"]
    mix = P.sb([128, 16, NLOC], BF16, "mix")
    for j in range(16):
        P.dma(mix[:, j, :], io["mixT"][j], queue=qd[j % 2])
    gate = P.sb([128, 2, D], F32, "gate_bc")
    for s in range(2):
        P.dma(gate[:, s, :], io["gate"][s:s + 1, :].partition_broadcast(128), queue=qd[s])
    if final:
        fnw = P.sb([128, D], F32, "fnw_bc")
        P.dma(fnw.v(), io["fnw"][0:1, :].partition_broadcast(128))
        st = rot_sb(P, 2, [128, 4], F32, "fstat")
    wstage = P.sb([128, 8192], F32, "wostage")
    wres = P.sb([128, 16, D], BF16, "wres")
    for cg in range(4):
        st_ = wstage
        sv = st_.v().rearrange("p (j n) -> p j n", j=16)
        src = io["w_out"][:, cg * 512:(cg + 1) * 512].rearrange("(j p) n -> p j n", p=128)
        P.dma(sv[:, 0:8, :], src[:, 0:8, :], queue="sp"); P.dma(sv[:, 8:16, :], src[:, 8:16, :], queue="act")
        P.copy(wres[:, :, cg * 512:(cg + 1) * 512], sv, eng="dve" if cg % 2 == 0 else "pool")
    ps_y = rot_ps(P, 4, [128, 512], F32, "ps_y")
    xt = rot_sb(P, 2, [128, D], F32, "oxt")
    tmp = rot_sb(P, 2, [128, D], F32, "otmpy")
    ntile = (TOK if final else NLOC) // 128
    for t in range(ntile):
        s = 0 if t < TOK // 128 else 1
        x = xt.next(); tm = tmp.next()
        P.dma(x.v(), io["x"][t * 128:(t + 1) * 128, :], queue=qd[t % 2])
        for cg in range(4):
            ps = ps_y.next()
            for j in range(16):
                P.mm(ps.v(), mix[:, j, t * 128:(t + 1) * 128], wres[:, j, cg * 512:(cg + 1) * 512], start=(j == 0), stop=(j == 15))
            P.tt(tm[:, cg * 512:(cg + 1) * 512], ps.v(), gate[:, s, cg * 512:(cg + 1) * 512], ALU.mult)
        P.tt(x.v(), x.v(), tm.v(), ALU.add, eng="pool")
        if not final:
            P.dma(io["xo"][t * 128:(t + 1) * 128, :], x.v(), queue=qd[(t + 1) % 2], is_output=True)
        else:
            ss = st.next(); o = tm
            P.act(tm.v(), x.v(), AF.Square, accum_out=ss[:, 0:1])
            P.ts(ss[:, 1:2], ss[:, 0:1], 1.0 / D, ALU.mult, EPS, ALU.add)
            P.act(ss[:, 2:3], ss[:, 1:2], AF.Sqrt)
            P.recip(ss[:, 3:4], ss[:, 2:3])
            P.stt(o.v(), x.v(), ss[:, 3:4], fnw.v(), ALU.mult, ALU.mult)
            P.dma(io["out"][t * 128:(t + 1) * 128, :], o.v(), queue=qd[(t + 1) % 2], is_output=True)


def build_out(final=False):
    P = Prog("outf" if final else "outp"); P.setup()
    spec_in = dict(OUT_IN)
    spec_out = dict(OUT_OUT)
    if final:
        spec_in["fnw"] = ([1, D], F32)
        spec_out = {"out": ([TOK, D], F32)}
    io = mk_io(P, {**spec_in, **spec_out}, set(spec_in), set(spec_out))
    phase_out(P, io, final)
    return P


OD_IN = 6416
NT128 = NLOC // 128
O1_IN = {"x": ([NLOC, D], F32), "ident": ([128, 128], F32), "cosT": ([64, TOK], F32), "sinT": ([64, TOK], F32),
         "rmT": ([64, 64], F32), "modT": ([128, 2 * 3 * KC], F32), "normT": ([128, KC], F32),
         "w_in": ([D, OD_IN], F32), "gate_b": ([1, 16], F32)}
O1_OUT = {"mqT": ([4, 128, NLOC], BF16), "mkT": ([4, 128, NLOC], BF16), "mk": ([NLOC, 512], BF16),
          "mva": ([NLOC, 4 * 257], BF16), "og": ([NLOC, 1024], F32), "gts": ([NLOC, 16], F32),
          "aqT": ([16, 64, NLOC], BF16), "akT": ([2, 64, NLOC], BF16), "av": ([NLOC, 128], BF16),
          "azT": ([16, 64, NLOC], BF16)}


def phase_o1(P, io):
    ident = P.sb([128, 128], F32, "ident")
    P.dma(ident.v(), io["ident"].v())
    cosT = P.sb([64, TOK], F32, "cosT"); sinT = P.sb([64, TOK], F32, "sinT"); rmT = P.sb([64, 64], F32, "rmT")
    P.dma(cosT.v(), io["cosT"].v()); P.dma(sinT.v(), io["sinT"].v(), queue="act"); P.dma(rmT.v(), io["rmT"].v())
    onec = P.sb([128, 1], F32, "onec"); P.memset(onec.v(), 1.0)
    gcol, scol, _ = load_mod_cols(P, io["modT"], io["normT"])
    hT = P.sb([128, KC, NLOC], BF16, "hT")
    ps_tr = rot_ps(P, 2, [128, 512], F32, "ps_tr")
    emit_hT(P, io["x"], (gcol, scol), hT, ident, ps_tr)
    ws = WStream(P)
    ps_mm = rot_ps(P, 3, [128, 512], F32, "ps_mm")
    ps_rot = rot_ps(P, 1, [64, 512], F32, "ps_rot")
    ostg_b = rot_sb(P, 3, [128, NLOC], BF16, "ostg_b")
    rtmp = rot_sb(P, 4, [64, 512], F32, "rtmp")
    w_in = io["w_in"]
    qd = ["sp", "act"]

    def w_in_view(c0, nc_):
        return w_in[:, c0:c0 + nc_].rearrange("(j p) n -> p j n", p=128)

    def proj_fm(wbf, cl, m, evac):
        for ti, (t0, n) in enumerate(TT):
            ps = ps_mm.next()
            for j in range(KC):
                P.mm(ps[0:m, 0:n], wbf[:, j, cl:cl + m], hT[:, j, t0:t0 + n], start=(j == 0), stop=(j == KC - 1))
            evac(ps[0:m, 0:n], ti, t0, n)

    qscale = float(128 ** -0.5)
    for gi in range(4):
        wbf = ws.load(w_in_view(gi * 256, 256), KC, 256)
        for c in range(2):
            ch = gi * 2 + c
            stg = ostg_b.next()
            if ch < 4:
                proj_fm(wbf, c * 128, 128, lambda ps, ti, t0, n, stg=stg: P.act(stg[:, t0:t0 + n], ps, AF.Copy, scale=qscale))
                P.dma(io["mqT"][ch], stg.v(), queue=qd[ch % 2])
            else:
                proj_fm(wbf, c * 128, 128, lambda ps, ti, t0, n, stg=stg: P.copy(stg[:, t0:t0 + n], ps, eng="dve"))
                P.dma(io["mkT"][ch - 4], stg.v(), queue=qd[ch % 2])
    def rope_head(wbf, cl, dst):
        stg = ostg_b.next()

        def ev(ps, ti, t0, n):
            if t0 < TOK:
                rf = rtmp.next()
                P.copy(rf[0:64, 0:n], ps, eng="act")
                emit_rope(P, stg[0:64, t0:t0 + n], rf[0:64, 0:n], n, t0, cosT, sinT, rmT, ps_rot, rtmp)
            else:
                P.copy(stg[0:64, t0:t0 + n], ps, eng="act")
        proj_fm(wbf, cl, 64, ev)
        P.dma(dst, stg[0:64, :])
    for gi in range(4):
        wbf = ws.load(w_in_view(4112 + gi * 256, 256), KC, 256)
        for c in range(4):
            rope_head(wbf, c * 64, io["aqT"][gi * 4 + c])
    wbf = ws.load(w_in_view(5136, 128), KC, 128)
    for c in range(2):
        rope_head(wbf, c * 64, io["akT"][c])
    for gi in range(4):
        wbf = ws.load(w_in_view(5392 + gi * 256, 256), KC, 256)
        for c in range(4):
            stg = ostg_b.next()
            proj_fm(wbf, c * 64, 64, lambda ps, ti, t0, n, stg=stg: P.act(stg[0:64, t0:t0 + n], ps, AF.Silu))
            P.dma(io["azT"][gi * 4 + c], stg[0:64, :], queue=qd[c % 2])

    def proj_tm(wbf, ncols, evac):
        for t in range(NT128):
            ps = ps_mm.next()
            for j in range(KC):
                P.mm(ps[:, 0:ncols], hT[:, j, t * 128:(t + 1) * 128], wbf[:, j, 0:ncols], start=(j == 0), stop=(j == KC - 1))
            evac(ps[:, 0:ncols], t)
    tstg_b = rot_sb(P, 2, [128, NT128, 256], BF16, "tstg_b")
    for gi in range(2):
        wbf = ws.load(w_in_view(512 + gi * 256, 256), KC, 256)
        stg = tstg_b.next()
        proj_tm(wbf, 256, lambda ps, t, stg=stg: P.copy(stg[:, t, :], ps, eng="dve" if t % 2 == 0 else "act"))
        P.dma(io["mk"][:, gi * 256:(gi + 1) * 256].rearrange("(t p) n -> p t n", p=128), stg.v(), queue=qd[gi % 2])
    vstg = rot_sb(P, 2, [128, NT128, 257], BF16, "vstg")
    mva4 = io["mva"].v().rearrange("(t p) (h c) -> p t h c", p=128, h=4)
    for hd in range(4):
        wbf = ws.load(w_in_view(1024 + hd * 256, 256), KC, 256)
        stg = vstg.next()
        P.memset(stg[:, :, 256:257], 1.0, eng="pool")
        proj_tm(wbf, 256, lambda ps, t, stg=stg: P.copy(stg[:, t, 0:256], ps, eng="dve" if t % 2 == 0 else "act"))
        P.dma(mva4[:, :, hd, :], stg.v(), queue=qd[hd % 2])
    ogstg = rot_sb(P, 2, [128, NT128, 256], F32, "ogstg")
    sg = rot_sb(P, 2, [128, 256], F32, "sgt"); sz = rot_sb(P, 2, [128, 256], F32, "szt")
    for gi in range(4):
        wo = ws.load(w_in_view(2048 + gi * 256, 256), KC, 256)
        wz = ws.load(w_in_view(3088 + gi * 256, 256), KC, 256)
        stg = ogstg.next()
        for t in range(NT128):
            pso = ps_mm.next(); psz = ps_mm.next()
            for j in range(KC):
                P.mm(pso[:, 0:256], hT[:, j, t * 128:(t + 1) * 128], wo[:, j, :], start=(j == 0), stop=(j == KC - 1))
            for j in range(KC):
                P.mm(psz[:, 0:256], hT[:, j, t * 128:(t + 1) * 128], wz[:, j, :], start=(j == 0), stop=(j == KC - 1))
            a = sg.next(); b = sz.next()
            P.act(a.v(), pso[:, 0:256], AF.Sigmoid)
            P.act(b.v(), psz[:, 0:256], AF.Silu)
            P.tt(stg[:, t, :], a.v(), b.v(), ALU.mult)
        P.dma(io["og"][:, gi * 256:(gi + 1) * 256].rearrange("(t p) n -> p t n", p=128), stg.v(), queue=qd[gi % 2])
    wbf = ws.load(w_in_view(5264, 128), KC, 128)
    stg = tstg_b.next()
    proj_tm(wbf, 128, lambda ps, t, stg=stg: P.copy(stg[:, t, 0:128], ps, eng="dve"))
    P.dma(io["av"].v().rearrange("(t p) n -> p t n", p=128), stg[:, :, 0:128])
    wbf = ws.load(w_in_view(3072, 16), KC, 16)
    gb = P.sb([128, 16], F32, "gb_bc")
    P.dma(gb.v(), io["gate_b"][0:1, :].partition_broadcast(128))
    G = P.sb([128, NT128, 16], F32, "Gstg")
    e_ = P.sb([128, NT128, 16], F32, "Gexp")
    proj_tm(wbf, 16, lambda ps, t: P.tt(G[:, t, :], ps, gb.v(), ALU.add))
    G4 = G.v().rearrange("p t (a b c) -> p t a b c", a=2, b=2)
    E4 = e_.v().rearrange("p t (a b c) -> p t a b c", a=2, b=2)
    for t in range(NT128):
        P.act(E4[:, t, :, 1, :], G4[:, t, :, 1, :], AF.Exp, scale=-1.0)
        P.act(E4[:, t, :, 1, :], E4[:, t, :, 1, :], AF.Ln, bias=onec[:, 0:1])
        P.ts(G4[:, t, :, 1, :], E4[:, t, :, 1, :], -1.0, ALU.mult)
    P.dma(io["gts"].v().rearrange("(t p) n -> p t n", p=128), G.v())


MCONST_IN = {"tri": ([2, 128, 128], F32), "neg": ([2, 128, 128], F32)}


def mlstm_consts():
    r = np.arange(128)
    tri_f = (r[:, None] <= r[None, :]).astype(np.float32)
    tri_b = (r[:, None] >= r[None, :]).astype(np.float32)
    neg_f = np.where(r[:, None] <= r[None, :], 0.0, -30000.0).astype(np.float32)
    neg_b = np.where(r[:, None] >= r[None, :], 0.0, -30000.0).astype(np.float32)
    return {"tri": np.stack([tri_f, tri_b]), "neg": np.stack([neg_f, neg_b])}


class MEnv:
    pass


def mlstm_setup(P, io, need_q=True):
    E = MEnv()
    E.tri = P.sb([128, 2, 128], F32, "tri"); E.neg = P.sb([128, 2, 128], F32, "neg")
    P.dma(E.tri.v(), io["tri"].v().rearrange("a p t -> p a t")); P.dma(E.neg.v(), io["neg"].v().rearrange("a p t -> p a t"), queue="act")
    E.ones = P.sb([128, 128], F32, "ones_f"); P.memset(E.ones.v(), 1.0)
    E.G = P.sb([128, NT128, 16], F32, "mG")
    P.dma(E.G.v(), io["gts"].v().rearrange("(t p) n -> p t n", p=128))
    E.k = P.sb([128, NT128, 512], BF16, "mk_tok")
    P.dma(E.k.v(), io["mk"].v().rearrange("(t p) n -> p t n", p=128), queue="act")
    E.va = P.sb([128, NT128, 4 * 257], BF16, "mva")
    P.dma(E.va[:, 0:5, :], io["mva"][0:640, :].rearrange("(t p) n -> p t n", p=128))
    P.dma(E.va[:, 5:10, :], io["mva"][640:1280, :].rearrange("(t p) n -> p t n", p=128), queue="act")
    E.kT = P.sb([128, 4, NLOC], BF16, "mkT")
    P.dma(E.kT.v(), io["mkT"].v().rearrange("h p t -> p h t"))
    if need_q:
        E.qT = P.sb([128, 4, NLOC], BF16, "mqT")
        P.dma(E.qT.v(), io["mqT"].v().rearrange("h p t -> p h t"), queue="act")
    E.ps_b = rot_ps(P, 2, [128, 512], F32, "mps_b")
    E.ps_qk = rot_ps(P, 1, [128, 512], F32, "mps_qk")
    E.ps_o = rot_ps(P, 2, [128, 512], F32, "mps_o")
    E.ps_c = rot_ps(P, 2, [128, 512], F32, "mps_c")
    E.LFb = rot_sb(P, 3, [128, 128], F32, "mLFb")
    E.bbs = rot_sb(P, 3, [128, 128], F32, "mbbs")
    E.DT = rot_sb(P, 3, [128, 128], F32, "mDT")
    E.ST = rot_sb(P, 3, [128, 128], BF16, "mST")
    E.eb = rot_sb(P, 3, [128, 128], F32, "meb")
    E.qs = rot_sb(P, 3, [128, 128], BF16, "mqs")
    E.cols = rot_sb(P, 6, [128, 8], F32, "mcols")
    E.vw = rot_sb(P, 3, [128, 257], BF16, "mvw")
    E.Cf = [[P.sb([128, 257], F32, f"Cf{h}{d}") for d in range(2)] for h in range(4)]
    E.Cb = [[P.sb([128, 257], BF16, f"Cb{h}{d}") for d in range(2)] for h in range(4)]
    E.etot = P.sb([128, 8], F32, "etot")
    return E


def mlstm_step(P, E, hd, dr, ch, mode, hsum=None, written=None):
    c = E.cols.next()
    lf = E.G[:, ch, 4 + 8 * dr + hd:5 + 8 * dr + hd]
    ig = E.G[:, ch, 8 * dr + hd:8 * dr + hd + 1]
    tri = E.tri[:, dr, :]; neg = E.neg[:, dr, :]
    lfb = E.LFb.next()
    P.ts(lfb.v(), E.ones.v(), lf, ALU.mult)
    pb = E.ps_b.next()
    P.mm(pb[:, 0:128], lfb.v(), tri)
    P.mm(pb[:, 256:257], tri, lf)
    last = 127 if dr == 0 else 0
    P.tt(c[:, 0:1], ig, pb[:, 256:257], ALU.subtract)
    P.copy(c[:, 1:2], pb[:, last:last + 1], eng="dve")
    P.act(c[:, 2:3], c[:, 0:1], AF.Exp, bias=c[:, 1:2])
    P.act(c[:, 3:4], c[:, 1:2], AF.Exp)
    Cf = E.Cf[hd][dr]; Cb = E.Cb[hd][dr]
    va = E.va[:, ch, hd * 257:(hd + 1) * 257]
    if mode == "B":
        bbs = E.bbs.next()
        P.tt(bbs.v(), pb[:, 0:128], neg, ALU.add)
        dt_ = E.DT.next()
        P.act(dt_.v(), bbs.v(), AF.Exp, bias=c[:, 0:1])
        pq = E.ps_qk.next()
        P.mm(pq[:, 0:128], E.kT[:, hd, ch * 128:(ch + 1) * 128], E.qT[:, hd, ch * 128:(ch + 1) * 128])
        st = E.ST.next()
        P.tt(st.v(), pq[:, 0:128], dt_.v(), ALU.mult)
        eb = E.eb.next()
        P.act(eb.v(), pb[:, 0:128], AF.Exp)
        qs = E.qs.next()
        P.tt(qs.v(), E.qT[:, hd, ch * 128:(ch + 1) * 128], eb.v(), ALU.mult, eng="pool")
        po = E.ps_o.next()
        P.mm(po[:, 0:257], st.v(), va, start=True, stop=False)
        P.mm(po[:, 0:257], qs.v(), Cb.v(), start=False, stop=True)
        P.act(c[:, 6:7], po[:, 256:257], AF.Abs)
        P.ts(c[:, 4:5], c[:, 6:7], 1.0, ALU.max)
        P.recip(c[:, 5:6], c[:, 4:5])
        dst = hsum[:, ch, hd * 256:(hd + 1) * 256]
        if (ch, hd) in written:
            P.stt(dst, po[:, 0:256], c[:, 5:6], dst, ALU.mult, ALU.add)
        else:
            P.act(dst, po[:, 0:256], AF.Copy, scale=c[:, 5:6])
            written.add((ch, hd))
    vw = E.vw.next()
    P.ts(vw.v(), va, c[:, 2:3], ALU.mult, eng="pool")
    pc = E.ps_c.next()
    P.mm(pc[:, 0:257], E.k[:, ch, hd * 128:(hd + 1) * 128], vw.v())
    P.stt(Cf.v(), Cf.v(), c[:, 3:4], pc[:, 0:257], ALU.mult, ALU.add)
    P.copy(Cb.v(), Cf.v(), eng="pool")
    if mode == "A":
        j = hd * 2 + dr
        P.tt(E.etot[:, j:j + 1], E.etot[:, j:j + 1], c[:, 3:4], ALU.mult, eng="pool")


O1B_IN = {k: O1_OUT[k] for k in ("mqT", "mkT", "mk", "mva", "gts")}
O1B_OUT = {"hcs": ([LC, 1024], F32), "CnC": ([8, 128, 257], F32), "CnL": ([8, 128, 257], F32), "eBT": ([128, 8], F32)}


def phase_o1b(P, io):
    E = mlstm_setup(P, io)
    hsum = P.sb([128, NT128, 1024], F32, "hsum")
    written = set()
    for hd in range(4):
        for dr in range(2):
            P.memset(E.Cf[hd][dr].v(), 0.0, eng="pool"); P.memset(E.Cb[hd][dr].v(), 0.0, eng="pool")
    P.memset(E.etot.v(), 1.0, eng="pool")
    for i in range(2):
        for hd in range(4):
            for dr in range(2):
                ch = 8 + (i if dr == 0 else 1 - i)
                mlstm_step(P, E, hd, dr, ch, "B", hsum, written)
    P.dma(io["hcs"].v().rearrange("(t p) n -> p t n", p=128), hsum[:, 8:10, :])
    for hd in range(4):
        for dr in range(2):
            P.dma(io["CnC"][hd * 2 + dr], E.Cf[hd][dr].v(), queue="act" if dr else "sp")
    for hd in range(4):
        for dr in range(2):
            P.memset(E.Cf[hd][dr].v(), 0.0, eng="pool"); P.memset(E.Cb[hd][dr].v(), 0.0, eng="pool")
    for i in range(8):
        for hd in range(4):
            for dr in range(2):
                ch = i if dr == 0 else 7 - i
                mlstm_step(P, E, hd, dr, ch, "A")
    for hd in range(4):
        for dr in range(2):
            P.dma(io["CnL"][hd * 2 + dr], E.Cf[hd][dr].v(), queue="act" if dr else "sp")
    P.dma(io["eBT"].v(), E.etot.v())


O2A_IN = {**O1B_IN, "og": O1_OUT["og"], "hcs": O1B_OUT["hcs"], "CnC": O1B_OUT["CnC"],
          "FC": ([8, 7, 128, 257], F32), "FE": ([128, 8 * 7], F32), "hnw": ([1, 1024], F32), "ident": ([128, 128], F32)}


def phase_o2a(P, io, need_ctx=True):
    E = mlstm_setup(P, io)
    hsum = P.sb([128, NT128, 1024], F32, "hsum")
    written = set()
    fe = P.sb([128, 8, 7], F32, "foldE")
    P.dma(fe.v().rearrange("p a b -> p (a b)"), io["FE"].v())
    fcb = rot_sb(P, 2, [128, 7, 257], F32, "foldC")
    for hd in range(4):
        for dr in range(2):
            j = hd * 2 + dr
            Cf = E.Cf[hd][dr]
            P.dma(Cf.v(), io["CnC"][j], queue="act")
            fc = fcb.next()
            P.dma(fc.v(), io["FC"][j].rearrange("s p n -> p s n"))
            for sl in range(7):
                P.stt(Cf.v(), Cf.v(), fe[:, j, sl:sl + 1], fc[:, sl, :], ALU.mult, ALU.add)
            P.copy(E.Cb[hd][dr].v(), Cf.v(), eng="pool")
    for i in range(8):
        for hd in range(4):
            for dr in range(2):
                ch = i if dr == 0 else 7 - i
                mlstm_step(P, E, hd, dr, ch, "B", hsum, written)
    nt = NT128 if need_ctx else TOK // 128
    if need_ctx:
        P.dma(hsum[:, 8:10, :], io["hcs"].v().rearrange("(t p) n -> p t n", p=128))
    ident = P.sb([128, 128], F32, "ident"); P.dma(ident.v(), io["ident"].v())
    identb = P.sb([128, 128], BF16, "identb"); P.copy(identb.v(), ident.v())
    hnw = P.sb([128, 1024], F32, "hnw_bc")
    P.dma(hnw.v(), io["hnw"][0:1, :].partition_broadcast(128))
    ogt = rot_sb(P, 2, [128, 1024], F32, "ogt")
    junk = P.sb([128, 256], BF16, "hjunk")
    st = rot_sb(P, 4, [128, 4], F32, "hstat")
    tmp = rot_sb(P, 2, [128, 256], F32, "htmp")
    ybf = rot_sb(P, 2, [128, 1024], BF16, "ybf")
    ps_t = rot_ps(P, 1, [128, 1024], BF16, "ps_ty")
    ystg = P.sb([128, 8, NLOC], BF16, "ystg")
    if not need_ctx:
        P.memset(ystg[:, :, TOK:], 0.0, eng="pool")
    for t in range(nt):
        og = ogt.next(); yb = ybf.next()
        P.dma(og.v(), io["og"][t * 128:(t + 1) * 128, :], queue="sp" if t % 2 == 0 else "act")
        for hd in range(4):
            ss = st.next(); tm = tmp.next()
            hv = hsum[:, t, hd * 256:(hd + 1) * 256]
            P.act(junk.v(), hv, AF.Square, accum_out=ss[:, 0:1])
            P.ts(ss[:, 1:2], ss[:, 0:1], 1.0 / 256, ALU.mult, EPS, ALU.add)
            P.act(ss[:, 2:3], ss[:, 1:2], AF.Sqrt)
            P.recip(ss[:, 3:4], ss[:, 2:3])
            P.stt(tm.v(), hv, ss[:, 3:4], hnw[:, hd * 256:(hd + 1) * 256], ALU.mult, ALU.mult)
            P.tt(yb[:, hd * 256:(hd + 1) * 256], tm.v(), og[:, hd * 256:(hd + 1) * 256], ALU.mult, eng="pool")
        pt = ps_t.next()
        for c in range(8):
            P.transpose(pt[:, c * 128:(c + 1) * 128], yb[:, c * 128:(c + 1) * 128], identb.v())
        P.copy(ystg[:, :, t * 128:(t + 1) * 128], pt.v().rearrange("p (c n) -> p c n", c=8), eng="act")
    for c in range(8):
        P.dma(io["mixT"][c], ystg[:, c, :], queue="sp" if c % 2 == 0 else "act")


O2B_IN = {"aqT": O1_OUT["aqT"], "azT": O1_OUT["azT"], "akx": ([2, 64, TOK + 256], BF16), "avx": ([TOK + 256, 128], BF16),
          "akc": ([2, 64, LC], BF16), "avc": ([LC, 128], BF16), "sink": ([1, 16], F32), "wm": ([4, 128, 128], F32)}


def phase_o2b(P, io, need_ctx=True):
    qd = ["sp", "act"]
    aq = P.sb([64, 16, NLOC], BF16, "aq"); az = P.sb([64, 16, NLOC], BF16, "az")
    for h in range(16):
        P.dma(aq[:, h, :], io["aqT"][h], queue=qd[h % 2]); P.dma(az[:, h, :], io["azT"][h], queue=qd[(h + 1) % 2])
    ak = P.sb([64, 2, TOK + 256], BF16, "akx"); akc = P.sb([64, 2, LC], BF16, "akc")
    P.dma(ak.v(), io["akx"].v().rearrange("g d t -> d g t")); P.dma(akc.v(), io["akc"].v().rearrange("g d t -> d g t"), queue="act")
    av = P.sb([128, 10, 128], BF16, "avx"); avc = P.sb([128, 2, 128], BF16, "avc")
    P.dma(av.v(), io["avx"].v().rearrange("(c p) n -> p c n", p=128)); P.dma(avc.v(), io["avc"].v().rearrange("(c p) n -> p c n", p=128), queue="act")
    ones = P.sb([128, 64], BF16, "ones64"); P.memset(ones.v(), 1.0)
    snk = P.sb([64, 16], F32, "sink_bc"); esnk = P.sb([64, 16], F32, "esink")
    P.dma(snk.v(), io["sink"][0:1, :].partition_broadcast(64))
    P.act(esnk.v(), snk.v(), AF.Exp)
    wmf = P.sb([128, 4, 128], F32, "wmf")
    P.dma(wmf.v(), io["wm"].v().rearrange("a p t -> p a t"))
    wm4 = P.sb([128, 4, 4, 128], BF16, "wm4")
    for kind in range(4):
        for rep in range(4):
            P.copy(wm4[:, kind, rep, :], wmf[:, kind, :], eng="pool")
    wst = P.sb([64, 16, NLOC], BF16, "wstage")
    ps_s = rot_ps(P, 3, [128, 512], F32, "wps_s")
    ps_o = rot_ps(P, 2, [64, 512], F32, "wps_o")
    ps_d = rot_ps(P, 2, [64, 512], F32, "wps_d")
    pT = rot_sb(P, 3, [128, 512], BF16, "wpT")
    dsb = rot_sb(P, 2, [64, 512], F32, "wden")
    osb = rot_sb(P, 2, [64, 512], F32, "wosb")
    scale = float(64 ** -0.5)
    nblk = 8 + (2 if need_ctx else 0)
    if not need_ctx:
        P.memset(wst[:, :, TOK:], 0.0, eng="pool")
    for n in range(nblk):
        for g in range(2):
            if n < 8:
                kinds = [(ak[:, g, n * 128:(n + 1) * 128], av[:, n, g * 64:(g + 1) * 64], 0 if n == 0 else 1),
                         (ak[:, g, (n + 1) * 128:(n + 2) * 128], av[:, n + 1, g * 64:(g + 1) * 64], None),
                         (ak[:, g, (n + 2) * 128:(n + 3) * 128], av[:, n + 2, g * 64:(g + 1) * 64], 3 if n == 7 else 2)]
            else:
                kinds = []
            for c in range(2):
                kinds.append((akc[:, g, c * 128:(c + 1) * 128], avc[:, c, g * 64:(g + 1) * 64], None))
            q0 = n * 128
            for half in range(2):
                h0 = g * 8 + half * 4
                po = ps_o.next(); pd = ps_d.next()
                def wscores(kv):
                    ps = ps_s.next()
                    P.mm(ps.v().rearrange("p (h q) -> p h q", h=4), kv, aq[:, h0:h0 + 4, q0:q0 + 128])
                    return ps
                ps_next = wscores(kinds[0][0])
                for ci, (kv, vv, mk) in enumerate(kinds):
                    ps = ps_next
                    if ci + 1 < len(kinds):
                        ps_next = wscores(kinds[ci + 1][0])
                    p = pT.next()
                    P.act(p.v(), ps.v(), AF.Exp, scale=scale)
                    if mk is not None:
                        P.tt(p.v(), p.v(), wm4[:, mk].rearrange("p h q -> p (h q)"), ALU.mult)
                    P.mm(po.v(), vv, p.v(), start=(ci == 0), stop=(ci == len(kinds) - 1))
                    P.mm(pd.v(), ones.v(), p.v(), start=(ci == 0), stop=(ci == len(kinds) - 1))
                ds = dsb.next(); os_ = osb.next()
                for hh in range(4):
                    P.ts(ds[:, hh * 128:(hh + 1) * 128], pd[:, hh * 128:(hh + 1) * 128], esnk[:, h0 + hh:h0 + hh + 1], ALU.add)
                P.recip(ds.v(), ds.v())
                P.tt(os_.v(), po.v(), ds.v(), ALU.mult)
                P.tt(wst[:, h0:h0 + 4, q0:q0 + 128], os_.v().rearrange("p (h q) -> p h q", h=4), az[:, h0:h0 + 4, q0:q0 + 128], ALU.mult, eng="pool")
    for h in range(16):
        P.dma(io["mixT"][8 + h // 2, (h % 2) * 64:(h % 2) * 64 + 64, :], wst[:, h, :], queue=qd[h % 2])


def window_masks(r):
    j = np.arange(128)[:, None]; i = np.arange(128)[None, :]
    prev = (j >= i).astype(np.float32); nxt = (j <= i).astype(np.float32)
    return np.stack([prev * (0.0 if r == 0 else 1.0), prev, nxt, nxt * (0.0 if r == NCORES - 1 else 1.0)])


def fold_lists(r, CnL_all, eBT_all):
    FC = np.zeros((8, 7, 128, 257), np.float32); FE = np.ones((128, 8, 7), np.float32)
    for hd in range(4):
        for dr in range(2):
            j = hd * 2 + dr
            segs = list(range(0, r)) if dr == 0 else list(range(NCORES - 1, r, -1))
            for sl, sg in enumerate(segs):
                FC[j, sl] = CnL_all[sg][j]; FE[:, j, sl] = eBT_all[sg][:, j]
    return FC, np.ascontiguousarray(FE.reshape(128, 56))


def o2b_halo(r, o1o):
    z = np.zeros((2, 64, 128), o1o[r]["akT"].dtype)
    zv = np.zeros((128, 128), o1o[r]["av"].dtype)
    kl = o1o[r - 1]["akT"][:, :, TOK - 128:TOK] if r > 0 else z
    kr = o1o[r + 1]["akT"][:, :, 0:128] if r < NCORES - 1 else z
    vl = o1o[r - 1]["av"][TOK - 128:TOK] if r > 0 else zv
    vr = o1o[r + 1]["av"][0:128] if r < NCORES - 1 else zv
    akx = np.concatenate([kl, o1o[r]["akT"][:, :, 0:TOK], kr], 2)
    avx = np.concatenate([vl, o1o[r]["av"][0:TOK], vr], 0)
    return (np.ascontiguousarray(akx), np.ascontiguousarray(avx), np.ascontiguousarray(o1o[r]["akT"][:, :, TOK:]),
            np.ascontiguousarray(o1o[r]["av"][TOK:]))


def _build(name, phases, ext_in, ext_out, specs):
    P = Prog(name); P.setup()
    io = mk_io(P, specs, set(ext_in), set(ext_out))
    for ph in phases:
        P.begin_phase()
        ph(P, io)
        P.end_phase()
    return P


def build_L_e1():
    return _build("e1", [phase_e1], E1_IN, E1_OUT, {**E1_IN, **E1_OUT})


def build_L_e2(final=False):
    spec_in = {**E2A_IN, **{k: v for k, v in OUT_IN.items() if k != "mixT"}}
    spec_out = dict(OUT_OUT)
    specs = {**spec_in, **E2A_OUT, **spec_out}
    return _build("e2", [phase_e2a, lambda P, io: phase_out(P, io, False)], spec_in, spec_out, specs)


def build_L_o1():
    ext_in = {**O1_IN, **MCONST_IN}
    ext_out = {**O1_OUT, **O1B_OUT}
    return _build("o1", [phase_o1, phase_o1b], ext_in, ext_out, {**ext_in, **ext_out})


def build_L_o2(need_ctx, final):
    spec_in = {**O2A_IN, **MCONST_IN, **O2B_IN, **{k: v for k, v in OUT_IN.items() if k != "mixT"}}
    spec_in = {k: v for k, v in spec_in.items()}
    if final:
        spec_in["fnw"] = ([1, D], F32)
        spec_out = {"out": ([TOK, D], F32)}
    else:
        spec_out = dict(OUT_OUT)
    specs = {**spec_in, "mixT": ([16, 128, NLOC], BF16), **spec_out}
    return _build("o2f" if final else "o2", [lambda P, io: phase_o2a(P, io, need_ctx), lambda P, io: phase_o2b(P, io, need_ctx),
                                             lambda P, io: phase_out(P, io, final)], spec_in, spec_out, specs)


def kernel(x, c, ctx, c_ctx, ada_w, ada_b, norm_w, ev_w_in, ev_q_norm, ev_kv_norm, ev_w_uq, ev_w_ukv,
           ev_w_pool, ev_pool_scale, ev_w_out, od_w_in, od_gate_b, od_head_norm, od_sink, od_w_out, final_norm):
    f32 = lambda a: np.ascontiguousarray(np.asarray(a, dtype=np.float32))
    x = f32(x); ctx = f32(ctx)
    mod = run_mods({"c": f32(c), "c_ctx": f32(c_ctx), "ada_w": f32(ada_w), "ada_b": f32(ada_b)})
    tabs = rope_tables()
    cosT, sinT, rmT = tabs
    ident = np.eye(128, dtype=np.float32)
    mconst = mlstm_consts()
    X = x[0]; XC = ctx[0]
    out = None
    for i in range(DEPTH):
        j = i // 2
        need_ctx = i < DEPTH - 1
        final = i == DEPTH - 1
        gate2 = np.ascontiguousarray(mod[i][:, 2 * D:3 * D])
        if i % 2 == 0:
            ins = [e1_inputs(r, X, XC, mod[i], f32(norm_w[i]), f32(ev_w_in[j]), f32(ev_q_norm[j]), f32(ev_kv_norm[j]),
                             f32(ev_w_uq[j]), f32(ev_w_ukv[j]), tabs) for r in range(NCORES)]
            e1o = run(build_L_e1(), ins)
            KN, KR, V = gather_kv(e1o)
            ins = []
            for r in range(NCORES):
                d = e2a_inputs(r, e1o, KN, KR, V, f32(ev_w_pool[j]), f32(ev_pool_scale[j]))
                d.update({"w_out": f32(ev_w_out[j]), "x": np.ascontiguousarray(np.concatenate([X[r * TOK:(r + 1) * TOK], XC], 0)),
                          "gate": gate2})
                ins.append(d)
            res = run(build_L_e2(), ins)
        else:
            ins = []
            for r in range(NCORES):
                d = {"x": np.ascontiguousarray(np.concatenate([X[r * TOK:(r + 1) * TOK], XC], 0)), "ident": ident,
                     "cosT": np.ascontiguousarray(cosT[:, r * TOK:(r + 1) * TOK]),
                     "sinT": np.ascontiguousarray(sinT[:, r * TOK:(r + 1) * TOK]), "rmT": rmT,
                     "modT": mod_cols(mod[i]), "normT": colT(f32(norm_w[i]), KC), "w_in": f32(od_w_in[j]),
                     "gate_b": f32(od_gate_b[j])[None, :]}
                d.update(mconst)
                ins.append(d)
            o1o = run(build_L_o1(), ins)
            CnL_all = [o["CnL"] for o in o1o]; eBT_all = [o["eBT"] for o in o1o]
            ins = []
            for r in range(NCORES):
                o = o1o[r]
                FC, FE = fold_lists(r, CnL_all, eBT_all)
                akx, avx, akc, avc = o2b_halo(r, o1o)
                d = {k: o[k] for k in O1B_IN}
                d.update({"og": o["og"], "hcs": o["hcs"], "CnC": o["CnC"], "FC": FC, "FE": FE,
                          "hnw": f32(od_head_norm[j])[None, :], "ident": ident,
                          "aqT": o["aqT"], "azT": o["azT"], "akx": akx, "avx": avx, "akc": akc, "avc": avc,
                          "sink": f32(od_sink[j])[None, :], "wm": window_masks(r),
                          "w_out": f32(od_w_out[j]), "x": np.ascontiguousarray(np.concatenate([X[r * TOK:(r + 1) * TOK], XC], 0)),
                          "gate": gate2})
                d.update(mconst)
                if final:
                    d["fnw"] = f32(final_norm)[None, :]
                ins.append(d)
            res = run(build_L_o2(need_ctx, final), ins)
        if final:
            out = np.concatenate([res[r]["out"] for r in range(NCORES)], 0)
        else:
            X = np.concatenate([res[r]["xo"][0:TOK] for r in range(NCORES)], 0)
            XC = np.ascontiguousarray(res[0]["xo"][TOK:])
    return np.ascontiguousarray(out[None].astype(np.float32))
```
